# Optimizing a Trainium2 kernel written in Bass

```python
import jax, jax.numpy as jnp
from jax import lax
import numpy as np

D_MODEL = 2048
BATCH = 4
SEQ = 2048
DEPTH = 2

N_BRANCH = 4
BRANCH_WIDTH = D_MODEL // 4
LRU_WIDTH = BRANCH_WIDTH
LRU_BLOCKS = 8
LRU_BLOCK_DIM = LRU_WIDTH // LRU_BLOCKS
LRU_CONV = 4
LRU_C = 8.0
POOL_WIDTH = BRANCH_WIDTH
POOL_WINDOWS = (2, 4, 8, 16)
POOL_GROUPS = 4
POOL_GROUP_DIM = POOL_WIDTH // POOL_GROUPS
SCONV_WIDTH = BRANCH_WIDTH
SCONV_K = 3
ATTN_HEADS = 8
HEAD_DIM = BRANCH_WIDTH // ATTN_HEADS
ATTN_WIDTH = ATTN_HEADS * HEAD_DIM
Q_BLOCK = 128
D_FF = 4 * D_MODEL
EPS = 1e-6
IN_SIZES = (LRU_WIDTH, POOL_WIDTH, 3 * SCONV_WIDTH, 3 * ATTN_WIDTH, ATTN_HEADS, N_BRANCH * D_MODEL)
N_IN = sum(IN_SIZES)

kernel_name = "hybrid_gated_parallel_mixers"


def _split_points():
    return [int(v) for v in np.cumsum(IN_SIZES)[:-1]]


def rms_norm(x, g):
    xf = x.astype(jnp.float32)
    y = xf * lax.rsqrt(jnp.mean(xf * xf, axis=-1, keepdims=True) + EPS)
    return (y * g.astype(jnp.float32)).astype(x.dtype)


def causal_depthwise_conv(x, w):
    k_width = w.shape[0]
    s = x.shape[1]
    xp = jnp.pad(x, ((0, 0), (k_width - 1, 0), (0, 0)))
    return sum(w[k] * xp[:, k:k + s] for k in range(k_width))


def rglru_branch(xa, conv_w, conv_b, wr, br, wi, bi, lam):
    b, s, _ = xa.shape
    u = causal_depthwise_conv(xa, conv_w) + conv_b
    ub = u.reshape(b, s, LRU_BLOCKS, LRU_BLOCK_DIM)
    r = jax.nn.sigmoid(jnp.einsum('bshi,hij->bshj', ub, wr).reshape(b, s, LRU_WIDTH) + br)
    gi = jax.nn.sigmoid(jnp.einsum('bshi,hij->bshj', ub, wi).reshape(b, s, LRU_WIDTH) + bi)
    log_a = (-LRU_C * r.astype(jnp.float32)) * jax.nn.softplus(-lam.astype(jnp.float32))
    a = jnp.exp(log_a)
    inp = jnp.sqrt(-jnp.expm1(2.0 * log_a)) * (gi * u).astype(jnp.float32)

    def combine(c1, c2):
        a1, b1 = c1
        a2, b2 = c2
        return a1 * a2, a2 * b1 + b2

    _, h = lax.associative_scan(combine, (a, inp), axis=1)
    return h.astype(xa.dtype)


def pool_branch(xp, w_grp, scale):
    b, s, _ = xp.shape
    xf = xp.astype(jnp.float32)
    csum = jnp.pad(jnp.cumsum(xf, axis=1), ((0, 0), (1, 0), (0, 0)))
    t = jnp.arange(s)
    outs = []
    for gidx, win in enumerate(POOL_WINDOWS):
        sl = slice(gidx * POOL_GROUP_DIM, (gidx + 1) * POOL_GROUP_DIM)
        start = jnp.maximum(t + 1 - win, 0)
        win_sum = csum[:, 1:, sl] - csum[:, start, sl]
        count = jnp.minimum(t + 1, win).astype(jnp.float32)[None, :, None]
        outs.append(win_sum / count - xf[:, :, sl])
    pooled = jnp.stack(outs, axis=2)
    mixed = jnp.einsum('bsgi,gij->bsgj', pooled, w_grp.astype(jnp.float32)).reshape(b, s, POOL_WIDTH)
    return (mixed * scale.astype(jnp.float32)).astype(xp.dtype)


def shortconv_branch(xsc, w):
    gate_b, gate_c, xc = jnp.split(xsc, 3, axis=-1)
    return gate_b * causal_depthwise_conv(gate_c * xc, w)


def forgetting_attention(qkv, f_logit, f_bias, q_g, k_g):
    b, s, _ = qkv.shape
    q, k, v = jnp.split(qkv, 3, axis=-1)
    q = rms_norm(q.reshape(b, s, ATTN_HEADS, HEAD_DIM), q_g)
    k = rms_norm(k.reshape(b, s, ATTN_HEADS, HEAD_DIM), k_g)
    v = v.reshape(b, s, ATTN_HEADS, HEAD_DIM)
    log_f = jax.nn.log_sigmoid((f_logit + f_bias).astype(jnp.float32))
    cum = jnp.cumsum(log_f, axis=1).transpose(0, 2, 1)
    n_blk = s // Q_BLOCK
    qb = q.reshape(b, n_blk, Q_BLOCK, ATTN_HEADS, HEAD_DIM).transpose(1, 0, 2, 3, 4)
    cqb = cum.reshape(b, ATTN_HEADS, n_blk, Q_BLOCK).transpose(2, 0, 1, 3)
    key_pos = jnp.arange(s)
    scale = HEAD_DIM ** -0.5

    def one_block(args):
        qi, cqi, blk = args
        logits = jnp.einsum('bqhd,bkhd->bhqk', qi, k).astype(jnp.float32) * scale
        logits = logits + (cqi[..., :, None] - cum[..., None, :])
        q_pos = blk * Q_BLOCK + jnp.arange(Q_BLOCK)
        logits = jnp.where(key_pos[None, :] <= q_pos[:, None], logits, -jnp.inf)
        p = jax.nn.softmax(logits, axis=-1)
        return jnp.einsum('bhqk,bkhd->bqhd', p.astype(v.dtype), v)

    out = lax.map(one_block, (qb, cqb, jnp.arange(n_blk)))
    return out.transpose(1, 0, 2, 3, 4).reshape(b, s, ATTN_WIDTH)


def setup_inputs(seed: int = 0) -> dict:
    key = jax.random.key(seed)
    ks = jax.random.split(key, 24)
    L, D, W = DEPTH, D_MODEL, BRANCH_WIDTH
    nrm = lambda k, shape, fan: jax.random.normal(k, shape, jnp.float32) * (fan ** -0.5)
    u = jax.random.uniform(ks[9], (L, LRU_WIDTH), jnp.float32, 0.9, 0.999)
    a0 = u ** (1.0 / LRU_C)
    return {
        "x": jax.random.normal(ks[0], (BATCH, SEQ, D), jnp.float32),
        "norm_mix_g": 1.0 + 0.1 * jax.random.normal(ks[1], (L, D), jnp.float32),
        "w_in": nrm(ks[2], (L, D, N_IN), D),
        "lru_conv_w": nrm(ks[3], (L, LRU_CONV, LRU_WIDTH), LRU_CONV),
        "lru_conv_b": 0.02 * jax.random.normal(ks[4], (L, LRU_WIDTH), jnp.float32),
        "lru_wr": nrm(ks[5], (L, LRU_BLOCKS, LRU_BLOCK_DIM, LRU_BLOCK_DIM), LRU_BLOCK_DIM),
        "lru_br": 0.02 * jax.random.normal(ks[6], (L, LRU_WIDTH), jnp.float32),
        "lru_wi": nrm(ks[7], (L, LRU_BLOCKS, LRU_BLOCK_DIM, LRU_BLOCK_DIM), LRU_BLOCK_DIM),
        "lru_bi": 0.02 * jax.random.normal(ks[8], (L, LRU_WIDTH), jnp.float32),
        "lru_lambda": jnp.log(a0) - jnp.log1p(-a0),
        "pool_w": nrm(ks[10], (L, POOL_GROUPS, POOL_GROUP_DIM, POOL_GROUP_DIM), POOL_GROUP_DIM),
        "pool_scale": 1.0 + 0.1 * jax.random.normal(ks[11], (L, POOL_WIDTH), jnp.float32),
        "sconv_w": nrm(ks[12], (L, SCONV_K, SCONV_WIDTH), SCONV_K),
        "q_norm_g": 1.0 + 0.1 * jax.random.normal(ks[13], (L, HEAD_DIM), jnp.float32),
        "k_norm_g": 1.0 + 0.1 * jax.random.normal(ks[14], (L, HEAD_DIM), jnp.float32),
        "forget_b": jax.random.uniform(ks[15], (L, ATTN_HEADS), jnp.float32, 1.0, 5.0),
        "w_branch": nrm(ks[16], (L, N_BRANCH, W, D), W),
        "w_out": nrm(ks[17], (L, D, D), D),
        "norm_mlp_g": 1.0 + 0.1 * jax.random.normal(ks[18], (L, D), jnp.float32),
        "w_mlp_up": nrm(ks[19], (L, D, D_FF), D),
        "w_mlp_down": nrm(ks[20], (L, D_FF, D), D_FF),
    }


def reference(x, norm_mix_g, w_in, lru_conv_w, lru_conv_b, lru_wr, lru_br, lru_wi, lru_bi,
              lru_lambda, pool_w, pool_scale, sconv_w, q_norm_g, k_norm_g, forget_b,
              w_branch, w_out, norm_mlp_g, w_mlp_up, w_mlp_down):
    b, s, d = x.shape
    split_pts = _split_points()
    for l in range(DEPTH):
        xn = rms_norm(x, norm_mix_g[l])
        proj = xn @ w_in[l]
        xa, xpool, xsc, qkv, f_logit, gate_logits = jnp.split(proj, split_pts, axis=-1)
        y_a = rglru_branch(xa, lru_conv_w[l], lru_conv_b[l], lru_wr[l], lru_br[l],
                           lru_wi[l], lru_bi[l], lru_lambda[l])
        y_b = pool_branch(xpool, pool_w[l], pool_scale[l])
        y_c = shortconv_branch(xsc, sconv_w[l])
        y_d = forgetting_attention(qkv, f_logit, forget_b[l], q_norm_g[l], k_norm_g[l])
        ys = jnp.stack([y_a, y_b, y_c, y_d], axis=2)
        branches = jnp.einsum('bskw,kwd->bskd', ys, w_branch[l])
        gates = jax.nn.sigmoid(gate_logits.reshape(b, s, N_BRANCH, d))
        merged = jnp.sum(gates * branches, axis=2)
        x = x + merged @ w_out[l]
        h = rms_norm(x, norm_mlp_g[l]) @ w_mlp_up[l]
        x = x + jnp.square(jax.nn.relu(h)) @ w_mlp_down[l]
    return x
```

```python
import contextlib
import numpy as np
import ml_dtypes
import concourse.bass as bass
import concourse.mybir as mybir
from concourse.bass_utils import run_bass_kernel_spmd

F32 = mybir.dt.float32
BF16 = mybir.dt.bfloat16
AF = mybir.ActivationFunctionType
ALU = mybir.AluOpType

D = 2048
T = 1024
TW = 512
NT = 2
DEPTH = 2
NPRM = 96
NCST = 72
EPS = 1e-6


class Tok:
    __slots__ = ("name", "grp", "lo", "hi", "writer", "readers")

    def __init__(self, name, grp=None, lo=0, hi=0):
        self.name = name
        self.grp = grp
        self.lo = lo
        self.hi = hi
        self.writer = None
        self.readers = {}


class Op:
    __slots__ = ("eng", "idx", "fn", "deps", "signal", "val", "dma_sem", "dma_val", "waits", "inc", "tag")


class Prog:
    ENGS = ("pe", "act", "dve", "pool", "sp")

    def __init__(self, nc):
        self.nc = nc
        self.ops = {e: [] for e in self.ENGS}
        self.groups = {}
        self.dma_count = {}
        self.sync_same = {"act": True, "dve": True, "pool": True, "pe": False, "sp": False}

    def tok(self, name, grp=None, lo=0, hi=0):
        t = Tok(name, grp, lo, hi)
        if grp is not None:
            self.groups.setdefault(grp, []).append(t)
        return t

    def _overl(self, t):
        if t.grp is None:
            return (t,)
        return [u for u in self.groups[t.grp] if u.lo < t.hi and t.lo < u.hi]

    def add(self, eng, fn, reads=(), writes=(), dma=None, inc=16):
        op = Op()
        op.eng = eng
        op.idx = len(self.ops[eng])
        op.fn = fn
        op.signal = False
        op.val = None
        op.dma_sem = dma
        op.dma_val = None
        op.waits = None
        op.inc = inc
        import sys as _sys
        fr = _sys._getframe(1)
        op.tag = (fr.f_lineno, fr.f_back.f_lineno if fr.f_back else 0)
        deps = set()
        for r in reads:
            for u in self._overl(r):
                if u.writer is not None:
                    deps.add(u.writer)
        for w in writes:
            for u in self._overl(w):
                if u.writer is not None:
                    deps.add(u.writer)
                for o in u.readers.values():
                    deps.add(o)
        op.deps = deps
        if dma is not None:
            self.dma_count[dma] = self.dma_count.get(dma, 0) + inc
            op.dma_val = self.dma_count[dma]
        for r in reads:
            key = eng if dma is None else ("dma", dma)
            r.readers[key] = op
        for w in writes:
            w.writer = op
            w.readers = {}
        self.ops[eng].append(op)
        return op

    def finalize(self):
        for e in self.ENGS:
            seen_eng = {}
            seen_dma = {}
            for op in self.ops[e]:
                best_e = {}
                best_d = {}
                for d in op.deps:
                    if d.dma_sem is not None:
                        if seen_dma.get(d.dma_sem, 0) >= d.dma_val:
                            continue
                        if best_d.get(d.dma_sem) is None or best_d[d.dma_sem].dma_val < d.dma_val:
                            best_d[d.dma_sem] = d
                    else:
                        if d.eng == e and (not self.sync_same[e] or d.idx >= op.idx):
                            continue
                        if seen_eng.get(d.eng, -1) >= d.idx:
                            continue
                        if best_e.get(d.eng) is None or best_e[d.eng].idx < d.idx:
                            best_e[d.eng] = d
                for k, d in best_e.items():
                    seen_eng[k] = d.idx
                    d.signal = True
                for k, d in best_d.items():
                    seen_dma[k] = d.dma_val
                op.waits = list(best_e.values()) + list(best_d.values())
        for e in self.ENGS:
            c = 0
            for op in self.ops[e]:
                if op.dma_sem is None and op.signal:
                    c += 1
                    op.val = c

    def emit(self, final_waits=()):
        nc = self.nc
        self.finalize()
        with contextlib.ExitStack() as st:
            esem = {e: st.enter_context(nc.semaphore("s_" + e)) for e in self.ENGS}
            dsem = {n: st.enter_context(nc.semaphore("d_" + n)) for n in self.dma_count}
            block = st.enter_context(nc.Block())
            binder = {"pe": block.tensor, "act": block.scalar, "dve": block.vector,
                      "pool": block.gpsimd, "sp": block.sync}

            def run(e, eng):
                for op in self.ops[e]:
                    for d in op.waits:
                        if d.dma_sem is not None:
                            eng.wait_ge(dsem[d.dma_sem], d.dma_val)
                        else:
                            eng.wait_ge(esem[d.eng], d.val)
                    ins = op.fn(eng)
                    import os as _os
                    if _os.environ.get("DBG_INS") and getattr(getattr(ins, "ins", None), "name", None) == _os.environ["DBG_INS"]:
                        print("DBG_INS", op.eng, op.idx, op.tag)
                    if op.dma_sem is not None:
                        ins.then_inc(dsem[op.dma_sem], op.inc)
                    elif op.signal:
                        ins.then_inc(esem[e], 1)
                if e == "sp":
                    for n in final_waits:
                        eng.wait_ge(dsem[n], self.dma_count[n])

            for e in self.ENGS:
                def mk(e):
                    def body(eng):
                        run(e, eng)
                    return body
                binder[e](mk(e))


def o_mm(out, lhsT, rhs, start, stop):
    return lambda e: e.matmul(out, lhsT, rhs, start=start, stop=stop)


def o_tr(out, in_, ident):
    return lambda e: e.matmul(out, in_, ident, start=True, stop=True, is_transpose=True)


def o_act(out, in_, func, bias=None, scale=None):
    def f(e):
        kw = {}
        if bias is not None:
            kw["bias"] = bias
        if scale is not None:
            kw["scale"] = scale
        return e.activation(out, in_, func, **kw)
    return f


def o_tt(out, a, b, op):
    return lambda e: e.tensor_tensor(out=out, in0=a, in1=b, op=op)


def o_ts(out, a, s1, op0, s2=None, op1=None):
    if op1 is None:
        return lambda e: e.tensor_scalar(out=out, in0=a, scalar1=s1, scalar2=None, op0=op0)
    return lambda e: e.tensor_scalar(out=out, in0=a, scalar1=s1, scalar2=s2, op0=op0, op1=op1)


def o_stt(out, in0, scalar, in1, op0, op1):
    return lambda e: e.scalar_tensor_tensor(out=out, in0=in0, scalar=scalar, in1=in1, op0=op0, op1=op1)


def o_scan(out, d0, d1, init, op0, op1):
    return lambda e: e.tensor_tensor_scan(out=out, data0=d0, data1=d1, initial=init, op0=op0, op1=op1)


def o_vcopy(out, in_):
    return lambda e: e.tensor_copy(out=out, in_=in_)


def o_acopy(out, in_):
    return lambda e: e.copy(out, in_)


def o_recip(out, in_):
    return lambda e: e.reciprocal(out=out, in_=in_)


def o_memset(ap, v):
    return lambda e: e.memset(ap, v)


def o_dma(out, in_):
    return lambda e: e.dma_start(out=out, in_=in_)


def chunk_order():
    specs = [("misc",)]
    for c in range(4):
        specs.append(("in", 0 + c * 128))
    for c in range(4):
        specs.append(("in", 512 + c * 128))
    for c in range(4):
        specs.append(("in", 1024 + 512 + c * 128))
        specs.append(("in", 1024 + 1024 + c * 128))
        specs.append(("in", 1024 + c * 128))
    for c in range(4):
        specs.append(("in", 2560 + 512 + c * 128))
    for c in range(4):
        specs.append(("in", 2560 + 1024 + c * 128))
    for c in range(4):
        specs.append(("in", 2560 + c * 128))
    for j in range(16):
        specs.append(("branch", j))
        for k in range(4):
            specs.append(("in", 4104 + k * 2048 + j * 128))
    for i in range(16):
        specs.append(("out", i))
    for q in range(4):
        for hc in range(16):
            specs.append(("up", q * 16 + hc))
        for i in range(16):
            specs.append(("down", q, i))
    return specs


NCH = len(chunk_order())


def pack_weights(inp):
    specs = chunk_order()
    W = np.zeros((DEPTH * NCH, 128, 16, 128), np.float32)

    def std(mat):
        return mat.reshape(16, 128, 128).transpose(1, 0, 2)

    for l in range(DEPTH):
        w_in = inp["w_in"][l]
        for n, s in enumerate(specs):
            dst = W[l * NCH + n]
            if s[0] == "misc":
                for ri, name in enumerate(("lru_wr", "lru_wi")):
                    wl = inp[name][l]
                    for c in range(4):
                        dst[0:64, ri * 4 + c, 0:64] = wl[2 * c]
                        dst[64:128, ri * 4 + c, 64:128] = wl[2 * c + 1]
                for g in range(4):
                    dst[:, 8 + g, :] = inp["pool_w"][l][g]
                wf = w_in[:, 4096:4104]
                dst[:, 12, :] = wf.reshape(16, 128, 8).transpose(1, 0, 2).reshape(128, 128)
            elif s[0] == "in":
                dst[:] = std(w_in[:, s[1]:s[1] + 128])
            elif s[0] == "branch":
                j = s[1]
                wb = inp["w_branch"][l][:, :, j * 128:(j + 1) * 128]
                dst[:] = wb.reshape(4, 4, 128, 128).transpose(2, 0, 1, 3).reshape(128, 16, 128)
            elif s[0] == "out":
                i = s[1]
                dst[:] = std(inp["w_out"][l][:, i * 128:(i + 1) * 128])
            elif s[0] == "up":
                j = s[1]
                dst[:] = std(inp["w_mlp_up"][l][:, j * 128:(j + 1) * 128])
            elif s[0] == "down":
                q, i = s[1], s[2]
                dst[:] = std(inp["w_mlp_down"][l][q * 2048:(q + 1) * 2048, i * 128:(i + 1) * 128])
    return W.reshape(DEPTH * NCH, 128, 2048)


def pack_params(inp):
    prm = np.zeros((DEPTH, 128, NPRM), np.float32)

    def col(v):
        return v.reshape(-1, 128).T

    for l in range(DEPTH):
        p = prm[l]
        p[:, 0:16] = col(inp["norm_mix_g"][l])
        p[:, 16:32] = col(inp["norm_mlp_g"][l])
        for k in range(4):
            p[:, 32 + k * 4:36 + k * 4] = col(inp["lru_conv_w"][l][k])
        p[:, 48:52] = col(inp["lru_conv_b"][l])
        p[:, 52:56] = col(inp["lru_br"][l])
        p[:, 56:60] = col(inp["lru_bi"][l])
        p[:, 60:64] = col(inp["lru_lambda"][l])
        p[:, 64:68] = col(inp["pool_scale"][l])
        for k in range(3):
            p[:, 68 + k * 4:72 + k * 4] = col(inp["sconv_w"][l][k])
        p[:, 80] = np.tile(inp["q_norm_g"][l], 2)
        p[:, 81] = np.tile(inp["k_norm_g"][l], 2)
        p[0:8, 82] = inp["forget_b"][l]
    return prm


def core_consts(half):
    cst = np.zeros((128, NCST), np.float32)
    cst[:, 0] = float(half)
    cst[:, 1] = 0.0 if half else -8.0e4
    cst[:, 2] = EPS
    cst[:, 3] = 1.0
    for g, win in enumerate((2, 4, 8, 16)):
        for t in range(16):
            cst[:, 4 + g * 16 + t] = 1.0 if half else float(win) / float(min(t + 1, win))
    return cst


def shared_consts():
    cm = np.zeros((128, 3, 128), np.float32)
    s = np.arange(128)[:, None]
    t = np.arange(128)[None, :]
    cm[:, 0, :] = (s <= t).astype(np.float32)
    cm[0:64, 1, 0:64] = 1.0
    cm[64:128, 1, 64:128] = 1.0
    cm[:, 2, :] = np.eye(128, dtype=np.float32)
    sel = np.zeros((8, 8, 128), np.float32)
    for h in range(8):
        sel[h, h, :] = 1.0
    return cm, sel


class _Stop(Exception):
    pass


def build(layers=(0, 1), debug=(), stop=None, ncores=8):
    nc = bass.Bass("TRN2", target_bir_lowering=False)
    nL = len(layers)
    xin = nc.dram_tensor("xT", [128, 16, T], F32, kind="ExternalInput").ap()
    Wd = nc.dram_tensor("W", [nL * NCH, 128, 16, 128], F32, kind="ExternalInput").ap()
    prmd = nc.dram_tensor("prm", [DEPTH, 128, NPRM], F32, kind="ExternalInput").ap()
    cstd = nc.dram_tensor("cst", [128, NCST], F32, kind="ExternalInput").ap()
    cmd = nc.dram_tensor("cmat", [128, 3, 128], F32, kind="ExternalInput").ap()
    seld = nc.dram_tensor("sel", [8, 8, 128], F32, kind="ExternalInput").ap()
    yout = nc.dram_tensor("yT", [128, 16, T], F32, kind="ExternalOutput").ap()
    xsp = nc.dram_tensor("xspill", [128, 16, T], F32, kind="Internal").ap()
    EXA = 522
    exa_s = [nc.dram_tensor(f"exa_s{l}", [EXA, 1024], BF16, kind="Internal") for l in range(nL)]
    exa_d = [nc.dram_tensor(f"exa_d{l}", [2 * EXA, 1024], BF16, kind="Internal") for l in range(nL)]
    exv_s = [nc.dram_tensor(f"exv_s{l}", [512, 1024], BF16, kind="Internal") for l in range(nL)]
    exv_d = [nc.dram_tensor(f"exv_d{l}", [1024, 1024], BF16, kind="Internal") for l in range(nL)]
    exb_s = [nc.dram_tensor(f"exb_s{l}", [17, 512], F32, kind="Internal") for l in range(nL)]
    exb_d = [nc.dram_tensor(f"exb_d{l}", [34, 512], F32, kind="Internal") for l in range(nL)]
    dbg_out = {}
    for name, shape, dt in debug:
        dbg_out[name] = nc.dram_tensor("dbg_" + name, list(shape), dt, kind="ExternalOutput").ap()
    groups = [[2 * g, 2 * g + 1] for g in range(ncores // 2)]

    def dap(th, off, dims):
        return bass.AP(th, off, [list(d) for d in dims])

    st = contextlib.ExitStack()
    with st:
        P = Prog(nc)

        def sb(name, shape, dt):
            return st.enter_context(nc.sbuf_tensor(name, list(shape), dt))

        R0 = sb("R0", [128, 16 * T + 64], F32)
        xn = sb("xn", [128, 16, T], BF16)
        R2 = sb("R2", [128, 16, T], BF16)
        NSLOT = 4
        wsl = [sb(f"w{i}", [128, 16, 128], BF16) for i in range(NSLOT)]
        wbr = [sb(f"wbr{i}", [128, 16, 128], BF16) for i in range(2)]
        NTMP = 12
        tmpf = [sb(f"tmp{i}", [128, 528], F32) for i in range(NTMP)]
        prm = sb("prm_sb", [128, nL, NPRM], F32)
        cst = sb("cst_sb", [128, NCST], F32)
        cmf = sb("cmf", [128, 3, 128], F32)
        cmb = sb("cmb", [128, 3, 128], BF16)
        self_ = sb("sel_sb", [8, 8, 128], F32)
        qa8 = sb("qa8", [8, T], F32)
        kcol = sb("kcol", [128, 16, 8], F32)
        small = sb("small", [128, 64], F32)
        tls = sb("tls", [128, 4, 20], BF16)
        tlr = sb("tlr", [128, 4, 20], BF16)

        R0b = R0[:].bitcast(BF16)
        xT = R0[:, 0:16 * T].rearrange("p (c t) -> p c t", c=16)
        ys = [R0b[:, k * 4096:(k + 1) * 4096].rearrange("p (c t) -> p c t", c=4) for k in range(4)]
        MB = 16384
        xa_pad = R0b[:, MB:MB + 4 * 1028].rearrange("p (c t) -> p c t", c=4)
        xp_pad = R0b[:, MB + 4112:MB + 4112 + 4 * 1040].rearrange("p (c t) -> p c t", c=4)
        u_pad = R0b[:, MB + 8272:MB + 8272 + 4 * 1028].rearrange("p (c t) -> p c t", c=4)
        gb_raw = R0b[:, MB + 12384:MB + 12384 + 4096].rearrange("p (c t) -> p c t", c=4)
        AB = MB
        kT_b = [R0b[:, AB + i * 2048:AB + (i + 1) * 2048] for i in range(2)]
        v_b = [R0b[:, AB + 4096 + i * 2048:AB + 4096 + (i + 1) * 2048].rearrange("p (b c) -> p b c", b=16) for i in range(2)]
        q_b = [R0b[:, AB + 8192 + i * 1024:AB + 8192 + (i + 1) * 1024] for i in range(2)]
        ct_b = [R0[:, 8192 + 5120 + i * 1024:8192 + 5120 + (i + 1) * 1024] for i in range(2)]
        pT_b = [R0b[:, AB + 14336 + i * 512:AB + 14336 + (i + 1) * 512] for i in range(4)]

        G = "R0"

        def tk(name, lo, hi):
            return P.tok(name, G, lo, hi)

        t_xT = [[tk(f"xT{c}_{n}", (c * T + n * TW) * 4, (c * T + n * TW + TW) * 4) for n in range(NT)] for c in range(16)]
        t_y = [[tk(f"y{k}_{c}", (k * 4096 + c * 1024) * 2, (k * 4096 + c * 1024 + 1024) * 2) for c in range(4)] for k in range(4)]
        b0 = MB * 2
        t_xa = [tk(f"xa{c}", b0 + c * 1028 * 2, b0 + (c + 1) * 1028 * 2) for c in range(4)]
        t_xp = [tk(f"xp{c}", b0 + (4112 + c * 1040) * 2, b0 + (4112 + (c + 1) * 1040) * 2) for c in range(4)]
        t_u = [tk(f"u{c}", b0 + (8272 + c * 1028) * 2, b0 + (8272 + (c + 1) * 1028) * 2) for c in range(4)]
        t_gb = [tk(f"gb{c}", b0 + (12384 + c * 1024) * 2, b0 + (12384 + (c + 1) * 1024) * 2) for c in range(4)]
        t_kT = [tk(f"kT{i}", b0 + i * 4096, b0 + (i + 1) * 4096) for i in range(2)]
        t_v = [tk(f"v{i}", b0 + 8192 + i * 4096, b0 + 8192 + (i + 1) * 4096) for i in range(2)]
        t_q = [tk(f"q{i}", b0 + 16384 + i * 2048, b0 + 16384 + (i + 1) * 2048) for i in range(2)]
        t_ct = [tk(f"ct{i}", (8192 + 5120 + i * 1024) * 4, (8192 + 5120 + (i + 1) * 1024) * 4) for i in range(2)]
        t_pT = [tk(f"pT{i}", b0 + (14336 + i * 512) * 2, b0 + (14336 + (i + 1) * 512) * 2) for i in range(4)]
        t_xn = [P.tok(f"xn{n}") for n in range(NT)]
        t_R2 = [[P.tok(f"R2_{c}_{n}") for n in range(NT)] for c in range(16)]
        t_w = [P.tok(f"w{i}") for i in range(NSLOT)]
        t_wbr = [P.tok(f"wbr{i}") for i in range(2)]
        t_tmp = [P.tok(f"tmp{i}") for i in range(NTMP)]
        t_prm, t_cst, t_cm, t_cmb, t_sel = P.tok("prm"), P.tok("cst"), P.tok("cm"), P.tok("cmb"), P.tok("sel")
        t_qa8, t_kcol, t_small, t_tls, t_tlr = P.tok("qa8"), P.tok("kcol"), P.tok("small"), P.tok("tls"), P.tok("tlr")
        t_xsp = P.tok("xsp")
        t_yout = P.tok("yout")
        t_exa_s = [P.tok(f"exa_s{l}") for l in range(nL)]
        t_exa_d = [P.tok(f"exa_d{l}") for l in range(nL)]
        t_exb_s = [P.tok(f"exb_s{l}") for l in range(nL)]
        t_exv_s = [P.tok(f"exv_s{l}") for l in range(nL)]
        t_exv_d = [P.tok(f"exv_d{l}") for l in range(nL)]
        t_exb_d = [P.tok(f"exb_d{l}") for l in range(nL)]

        psb = [st.enter_context(nc.psum_tensor(f"ps{i}", [128, 512], F32)) for i in range(8)]
        t_ps = [P.tok(f"ps{i}") for i in range(8)]
        ps_ctr = [0]

        def PS():
            i = ps_ctr[0] % 8
            ps_ctr[0] += 1
            return psb[i], t_ps[i]

        lo_ctr = [0]

        def PSLO():
            i = lo_ctr[0] % 4
            lo_ctr[0] += 1
            return psb[i], t_ps[i]

        hi_ctr = [0]

        def PSHI():
            i = 4 + (hi_ctr[0] % 4)
            hi_ctr[0] += 1
            return psb[i], t_ps[i]

        tmp_ctr = [0]

        def TMP():
            i = tmp_ctr[0] % NTMP
            tmp_ctr[0] += 1
            return tmpf[i], t_tmp[i]

        w_ctr = [0]
        specs = chunk_order()

        slot_ctr = [0]
        br_ctr = [0]

        def load_w(layer, expect):
            n = w_ctr[0]
            li = n // NCH
            assert layers[li] == layer and specs[n % NCH][0] == expect[0] and tuple(specs[n % NCH][1:]) == tuple(expect[1:]), (n, specs[n % NCH], expect)
            w_ctr[0] += 1
            if expect[0] == "branch":
                s = br_ctr[0] % 2
                br_ctr[0] += 1
                P.add("pool", o_dma(wbr[s][:], Wd[li * NCH + (n % NCH)]), writes=[t_wbr[s]], dma=f"wbr{s}")
                return wbr[s], t_wbr[s]
            s = slot_ctr[0] % NSLOT
            slot_ctr[0] += 1
            P.add("pool", o_dma(wsl[s][:], Wd[li * NCH + (n % NCH)]), writes=[t_w[s]], dma=f"w{s}")
            return wsl[s], t_w[s]

        P.add("sp", o_dma(prm[:], prmd[layers[0]:layers[0] + nL].rearrange("l p n -> p l n")), writes=[t_prm], dma="c0")
        P.add("sp", o_dma(cst[:], cstd), writes=[t_cst], dma="c1")
        P.add("sp", o_dma(cmf[:], cmd), writes=[t_cm], dma="c2")
        P.add("sp", o_dma(self_[:], seld), writes=[t_sel], dma="c3")
        P.add("dve", o_vcopy(cmb[:, 0:2, :], cmf[:, 0:2, :]), reads=[t_cm], writes=[t_cmb])
        P.add("dve", o_memset(cmb[:, 2, :], 1.0), writes=[t_cmb])
        mask_bf = cmb[:, 0, :]
        bdones = cmb[:, 1, :]
        ones_bf = cmb[:, 2, :]
        ident = cmf[:, 2, :]
        flag = cst[:, 0:1]
        negbig8 = cst[:, 1:2]
        eps_c = cst[:, 2:3]
        one_c = cst[:, 3:4]

        for c4 in range(4):
            P.add("sp", o_dma(xT[:, c4 * 4:(c4 + 1) * 4, :], xin[:, c4 * 4:(c4 + 1) * 4, :]),
                  writes=[t_xT[c][n] for c in range(c4 * 4, c4 * 4 + 4) for n in range(NT)], dma=f"x{c4}")

        def rmsnorm_to_xn(li, gcol0):
            for n in range(NT):
                ps, tp = PS()
                for c in range(16):
                    sq, tsq = TMP()
                    sqb = sq[:, 0:256].bitcast(BF16)
                    P.add("act", o_act(sqb, xT[:, c, n * TW:(n + 1) * TW], AF.Square), reads=[t_xT[c][n]], writes=[tsq])
                    P.add("pe", o_mm(ps[:, :], ones_bf, sqb, c == 0, c == 15), reads=[tsq, t_cmb], writes=[tp])
                sd, tsd = TMP()
                P.add("act", o_act(sd[:, 0:TW], ps[:, :], AF.Sqrt, bias=eps_c, scale=1.0 / D), reads=[tp, t_cst], writes=[tsd])
                rs, trs = TMP()
                P.add("dve", o_recip(rs[:, 0:TW], sd[:, 0:TW]), reads=[tsd], writes=[trs])
                for c in range(16):
                    P.add("dve", o_stt(xn[:, c, n * TW:(n + 1) * TW], xT[:, c, n * TW:(n + 1) * TW],
                                       prm[:, li, gcol0 + c:gcol0 + c + 1], rs[:, 0:TW], ALU.mult, ALU.mult),
                          reads=[t_xT[c][n], trs, t_prm], writes=[t_xn[n]])

        def proj_fm(layer, expect, evac):
            w, tw = load_w(layer, expect)
            for n in range(NT):
                ps, tp = PS()
                for kc in range(16):
                    P.add("pe", o_mm(ps[:, :], w[:, kc, :], xn[:, kc, n * TW:(n + 1) * TW], kc == 0, kc == 15),
                          reads=[tw, t_xn[n]], writes=[tp])
                evac(n, ps, tp)

        dbg_list = []

        def dbg(name, ap, toks):
            if name in dbg_out:
                dbg_list.append(name)
                P.add("sp", o_dma(dbg_out[name], ap), reads=list(toks), writes=[P.tok("dbg_" + name)], dma="dbg_" + name)

        def layer_body(li, layer):
            pl = prm[:, li, :]

            def pc(i):
                return prm[:, li, i:i + 1]

            rmsnorm_to_xn(li, 0)
            P.add("sp", o_dma(xsp, xT), reads=[t_xT[c][n] for c in range(16) for n in range(NT)], writes=[t_xsp], dma="spill")

            if stop == 'norm':
                raise _Stop()
            wm, twm = load_w(layer, ("misc",))
            wmisc = R2[:, 0:2, :].rearrange("p a t -> p (a t)").rearrange("p (s c) -> p s c", s=16)
            t_wmisc = t_R2[0][0]
            t_wm_all = [t_R2[0][0], t_R2[0][1], t_R2[1][0], t_R2[1][1]]
            P.add("dve", o_vcopy(wmisc, wm[:]), reads=[twm], writes=t_wm_all)
            P.add("act", o_act(small[:, 0:4], pl[:, 60:64], AF.Exp, scale=-1.0), reads=[t_prm], writes=[t_small])
            P.add("act", o_act(small[:, 0:4], small[:, 0:4], AF.Ln, bias=one_c), reads=[t_small, t_cst], writes=[t_small])
            P.add("dve", o_ts(small[:, 0:4], small[:, 0:4], -8.0, ALU.mult), reads=[t_small], writes=[t_small])
            P.add("dve", o_ts(small[:, 4:5], pl[:, 82:83], -1.0, ALU.mult), reads=[t_prm, t_small], writes=[t_small])
            P.add("dve", o_memset(xa_pad[:, :, 0:4], 0.0), writes=t_xa)
            P.add("dve", o_memset(xp_pad[:, :, 0:16], 0.0), writes=t_xp)
            P.add("dve", o_memset(u_pad[:, :, 0:4], 0.0), writes=t_u)

            def lru_chunk(c, final):
                prev_h = None
                for n in range(NT):
                    u, tu = TMP()
                    base = 1 + n * TW
                    P.add("dve", o_ts(u[:, 0:TW], xa_pad[:, c, base:base + TW], pc(32 + c), ALU.mult, pc(48 + c), ALU.add),
                          reads=[t_xa[c], t_prm], writes=[tu])
                    for k in range(1, 4):
                        P.add("dve", o_stt(u[:, 0:TW], xa_pad[:, c, base + k:base + k + TW], pc(32 + k * 4 + c), u[:, 0:TW], ALU.mult, ALU.add),
                              reads=[t_xa[c], t_prm, tu], writes=[tu])
                    ub, tub = TMP()
                    ubb = ub[:, 0:256].bitcast(BF16)
                    P.add("act", o_acopy(ubb, u[:, 0:TW]), reads=[tu], writes=[tub])
                    psr, tpr = PS()
                    P.add("pe", o_mm(psr[:, :], wmisc[:, c, :], ubb, True, True), reads=[tub, t_wmisc], writes=[tpr])
                    psi, tpi = PS()
                    P.add("pe", o_mm(psi[:, :], wmisc[:, 4 + c, :], ubb, True, True), reads=[tub, t_wmisc], writes=[tpi])
                    r, tr_ = TMP()
                    P.add("act", o_act(r[:, 0:TW], psr[:, :], AF.Sigmoid, bias=pc(52 + c)), reads=[tpr, t_prm], writes=[tr_])
                    gi, tgi = TMP()
                    P.add("act", o_act(gi[:, 0:TW], psi[:, :], AF.Sigmoid, bias=pc(56 + c)), reads=[tpi, t_prm], writes=[tgi])
                    P.add("act", o_act(r[:, 0:TW], r[:, 0:TW], AF.Exp, scale=small[:, c:c + 1]), reads=[tr_, t_small], writes=[tr_])
                    sq, tsq = TMP()
                    P.add("dve", o_tt(sq[:, 0:TW], r[:, 0:TW], r[:, 0:TW], ALU.mult), reads=[tr_], writes=[tsq])
                    P.add("act", o_act(sq[:, 0:TW], sq[:, 0:TW], AF.Sqrt, bias=one_c, scale=-1.0), reads=[tsq, t_cst], writes=[tsq])
                    P.add("dve", o_tt(gi[:, 0:TW], gi[:, 0:TW], u[:, 0:TW], ALU.mult), reads=[tgi, tu], writes=[tgi])
                    P.add("dve", o_tt(gi[:, 0:TW], gi[:, 0:TW], sq[:, 0:TW], ALU.mult), reads=[tgi, tsq], writes=[tgi])
                    h, th = TMP()
                    if n == 0:
                        init = small[:, 8 + c:9 + c] if final else 0.0
                        rd = [t_small] if final else []
                    else:
                        init = prev_h[0][:, TW - 1:TW]
                        rd = [prev_h[1]]
                    P.add("dve", o_scan(h[:, 0:TW], r[:, 0:TW], gi[:, 0:TW], init, ALU.mult, ALU.add),
                          reads=[tr_, tgi] + rd, writes=[th])
                    prev_h = (h, th)
                    if final:
                        P.add("act", o_acopy(ys[0][:, c, n * TW:(n + 1) * TW], h[:, 0:TW]), reads=[th], writes=[t_y[0][c]])
                    elif n == NT - 1:
                        P.add("act", o_acopy(small[:, 12 + c:13 + c], h[:, TW - 1:TW]), reads=[th], writes=[t_small])

            for c in range(4):
                def ev(n, ps, tp, c=c):
                    P.add("act", o_acopy(xa_pad[:, c, 4 + n * TW:4 + (n + 1) * TW], ps[:, :]), reads=[tp], writes=[t_xa[c]])
                proj_fm(layer, ("in", c * 128), ev)
                lru_chunk(c, False)
            for c in range(4):
                def ev(n, ps, tp, c=c):
                    P.add("act", o_acopy(xp_pad[:, c, 16 + n * TW:16 + (n + 1) * TW], ps[:, :]), reads=[tp], writes=[t_xp[c]])
                proj_fm(layer, ("in", 512 + c * 128), ev)
            for c in range(4):
                gcs = []

                def ev_gc(n, ps, tp):
                    g_, tg_ = TMP()
                    P.add("act", o_acopy(g_[:, 0:TW], ps[:, :]), reads=[tp], writes=[tg_])
                    gcs.append((g_, tg_))
                proj_fm(layer, ("in", 1536 + c * 128), ev_gc)

                def ev_xc(n, ps, tp, c=c):
                    g_, tg_ = gcs[n]
                    P.add("dve", o_tt(u_pad[:, c, 4 + n * TW:4 + (n + 1) * TW], ps[:, :], g_[:, 0:TW], ALU.mult),
                          reads=[tp, tg_], writes=[t_u[c]])
                proj_fm(layer, ("in", 2048 + c * 128), ev_xc)

                def ev_gb(n, ps, tp, c=c):
                    P.add("act", o_acopy(gb_raw[:, c, n * TW:(n + 1) * TW], ps[:, :]), reads=[tp], writes=[t_gb[c]])
                proj_fm(layer, ("in", 1024 + c * 128), ev_gb)

            if stop == 'projabc':
                raise _Stop()
            exs, exd = exa_s[li], exa_d[li]
            for c in range(4):
                kst, tkst = TMP()
                kstb = kst[:, 0:512].bitcast(BF16)

                def ev_k(n, ps, tp, c=c, kstb=kstb, tkst=tkst):
                    sq, tsq = TMP()
                    sqb = sq[:, 0:256].bitcast(BF16)
                    P.add("act", o_act(sqb, ps[:, :], AF.Square), reads=[tp], writes=[tsq])
                    ps2, tp2 = PS()
                    P.add("pe", o_mm(ps2[:, :], bdones, sqb, True, True), reads=[tsq, t_cmb], writes=[tp2])
                    sd, tsd = TMP()
                    P.add("act", o_act(sd[:, 0:TW], ps2[:, :], AF.Sqrt, bias=eps_c, scale=1.0 / 64), reads=[tp2, t_cst], writes=[tsd])
                    P.add("dve", o_recip(sd[:, 0:TW], sd[:, 0:TW]), reads=[tsd], writes=[tsd])
                    P.add("dve", o_stt(kstb[:, n * TW:(n + 1) * TW], ps[:, :], pc(81), sd[:, 0:TW], ALU.mult, ALU.mult),
                          reads=[tp, tsd, t_prm], writes=[tkst])
                proj_fm(layer, ("in", 2560 + 512 + c * 128), ev_k)
                P.add("sp", o_dma(dap(exs, c * 128 * 1024, [[1024, 128], [1, 1024]]), kstb), reads=[tkst], writes=[t_exa_s[li]], dma="exsta")

            for c in range(4):
                w, tw = load_w(layer, ("in", 2560 + 1024 + c * 128))
                vst, tvst = TMP()
                vstb = vst[:, 0:512].bitcast(BF16).rearrange("p (b c) -> p b c", b=8)
                for half in range(2):
                    ps, tp = PS()
                    for tb in range(4):
                        blk = half * 4 + tb
                        n = blk // 4
                        for kc in range(16):
                            P.add("pe", o_mm(ps[:, tb * 128:(tb + 1) * 128], xn[:, kc, blk * 128:(blk + 1) * 128], w[:, kc, :], kc == 0, kc == 15),
                                  reads=[tw, t_xn[n]], writes=[tp])
                    P.add("act", o_acopy(vstb[:, half * 4:(half + 1) * 4, :], ps[:, :].rearrange("p (b c) -> p b c", b=4)), reads=[tp], writes=[tvst])
                P.add("sp", o_dma(dap(exv_s[li], c * 8 * 128 * 128, [[128, 128], [128 * 128, 8], [1, 128]]), vstb),
                      reads=[tvst], writes=[t_exv_s[li]], dma="exstv")

            prev_S = None
            psk, tpk = PS()
            for n in range(NT):
                ps, tp = PS()
                wf = wmisc[:, 12, :].rearrange("p (k c) -> p k c", c=8)
                for kc in range(16):
                    P.add("pe", o_mm(ps[0:8, :], wf[:, kc, :], xn[:, kc, n * TW:(n + 1) * TW], kc == 0, kc == 15),
                          reads=[t_wmisc, t_xn[n]], writes=[tp])
                e_, te_ = TMP()
                P.add("act", o_act(e_[0:8, 0:TW], ps[0:8, :], AF.Exp, bias=small[0:8, 4:5], scale=-1.0), reads=[tp, t_small], writes=[te_])
                P.add("act", o_act(e_[0:8, 0:TW], e_[0:8, 0:TW], AF.Ln, bias=cst[0:8, 3:4]), reads=[te_, t_cst], writes=[te_])
                S, tS = TMP()
                if n == 0:
                    init, rd = 0.0, []
                else:
                    init, rd = prev_S[0][0:8, TW - 1:TW], [prev_S[1]]
                P.add("dve", o_scan(S[0:8, 0:TW], e_[0:8, 0:TW], e_[0:8, 0:TW], init, ALU.add, ALU.max), reads=[te_] + rd, writes=[tS])
                prev_S = (S, tS)
                P.add("dve", o_ts(qa8[:, n * TW:(n + 1) * TW], S[0:8, 0:TW], -8.0, ALU.mult), reads=[tS], writes=[t_qa8])
                k8, tk8 = TMP()
                P.add("dve", o_ts(k8[0:8, 0:TW], S[0:8, 0:TW], 8.0, ALU.mult), reads=[tS], writes=[tk8])
                for b4 in range(4):
                    blk = n * 4 + b4
                    P.add("pe", o_tr(psk[:, blk * 8:(blk + 1) * 8], k8[0:8, b4 * 128:(b4 + 1) * 128], ident[0:8, 0:8]),
                          reads=[tk8, t_cm], writes=[tpk])
                P.add("sp", o_dma(dap(exb_s[li], n * TW, [[1024, 8], [1, TW]]), S[0:8, 0:TW]), reads=[tS], writes=[t_exb_s[li]], dma="exstb")
            P.add("act", o_acopy(kcol[:, 8:16, :], psk[:, 0:64].rearrange("p (b h) -> p b h", b=8)), reads=[tpk], writes=[t_kcol])

            P.add("dve", o_vcopy(tls[:, :, 0:3], xa_pad[:, :, 1025:1028]), reads=t_xa, writes=[t_tls])
            P.add("dve", o_vcopy(tls[:, :, 3:18], xp_pad[:, :, 1025:1040]), reads=t_xp, writes=[t_tls])
            P.add("dve", o_vcopy(tls[:, :, 18:20], u_pad[:, :, 1026:1028]), reads=t_u, writes=[t_tls])
            P.add("sp", o_dma(dap(exs, 512 * 1024, [[80, 128], [1, 80]]), tls[:].rearrange("p c i -> p (c i)")), reads=[t_tls], writes=[t_exa_s[li]], dma="exsta")
            P.add("sp", o_dma(dap(exb_s[li], 8 * 1024, [[4, 128], [1, 4]]), small[:, 12:16]), reads=[t_small], writes=[t_exb_s[li]], dma="exstb")

            if stop == 'proj':
                raise _Stop()
            P.add("pool", (lambda exs=exs, exd=exd: lambda e: e.collective_compute("AllGather", ALU.bypass, replica_groups=groups, ins=[exs.ap()], outs=[exd.ap()]))(),
                  reads=[t_exa_s[li]], writes=[t_exa_d[li]], dma=f"cca{li}", inc=1)
            P.add("pool", (lambda a=exv_s[li], b=exv_d[li]: lambda e: e.collective_compute("AllGather", ALU.bypass, replica_groups=groups, ins=[a.ap()], outs=[b.ap()]))(),
                  reads=[t_exv_s[li]], writes=[t_exv_d[li]], dma=f"ccv{li}", inc=1)
            P.add("pool", (lambda a=exb_s[li], b=exb_d[li]: lambda e: e.collective_compute("AllGather", ALU.bypass, replica_groups=groups, ins=[a.ap()], outs=[b.ap()]))(),
                  reads=[t_exb_s[li]], writes=[t_exb_d[li]], dma=f"ccb{li}", inc=1)

            if stop == 'exch':
                raise _Stop()
            P.add("sp", o_dma(tlr[:].rearrange("p c i -> p (c i)"), dap(exd, 512 * 1024, [[80, 128], [1, 80]])), reads=[t_exa_d[li]], writes=[t_tlr], dma="exld_t")
            P.add("sp", o_dma(small[:, 20:24], dap(exb_d[li], 8 * 1024, [[4, 128], [1, 4]])), reads=[t_exb_d[li]], writes=[t_small], dma="exld_h")
            P.add("dve", o_ts(small[:, 8:12], small[:, 20:24], flag, ALU.mult), reads=[t_small, t_cst], writes=[t_small])
            P.add("dve", o_ts(xa_pad[:, :, 1:4], tlr[:, :, 0:3], flag, ALU.mult), reads=[t_tlr, t_cst], writes=t_xa)
            P.add("dve", o_ts(xp_pad[:, :, 1:16], tlr[:, :, 3:18], flag, ALU.mult), reads=[t_tlr, t_cst], writes=t_xp)
            P.add("dve", o_ts(u_pad[:, :, 2:4], tlr[:, :, 18:20], flag, ALU.mult), reads=[t_tlr, t_cst], writes=t_u)
            Sp = []
            for n in range(NT):
                s_, ts_ = TMP()
                P.add("sp", o_dma(s_[0:8, 0:TW], dap(exb_d[li], n * TW, [[1024, 8], [1, TW]])), reads=[t_exb_d[li]], writes=[ts_], dma=f"exld_s{n}")
                Sp.append((s_, ts_))
            psk2, tpk2 = PS()
            for n in range(NT):
                s_, ts_ = Sp[n]
                k8, tk8 = TMP()
                P.add("dve", o_ts(k8[0:8, 0:TW], s_[0:8, 0:TW], Sp[1][0][0:8, TW - 1:TW], ALU.subtract, 8.0, ALU.mult),
                      reads=[ts_, Sp[1][1]], writes=[tk8])
                P.add("dve", o_ts(k8[0:8, 0:TW], k8[0:8, 0:TW], negbig8[0:8, :], ALU.add), reads=[tk8, t_cst], writes=[tk8])
                for b4 in range(4):
                    blk = n * 4 + b4
                    P.add("pe", o_tr(psk2[:, blk * 8:(blk + 1) * 8], k8[0:8, b4 * 128:(b4 + 1) * 128], ident[0:8, 0:8]),
                          reads=[tk8, t_cm], writes=[tpk2])
            P.add("act", o_acopy(kcol[:, 0:8, :], psk2[:, 0:64].rearrange("p (b h) -> p b h", b=8)), reads=[tpk2], writes=[t_kcol])

            if stop == 'recv':
                raise _Stop()
            for c in range(4):
                lru_chunk(c, True)
            dbg("ya", ys[0], t_y[0])

            for g in range(4):
                win = 2 << g
                for n in range(NT):
                    lo = 16 + n * TW - 15
                    L = TW + 15
                    cur = (xp_pad[:, g, lo:lo + L], t_xp[g], 0)
                    sh = 1
                    for step in range(g + 1):
                        s_, ts_ = TMP()
                        ap, tk_, vf = cur
                        nv = vf + sh
                        P.add("dve", o_tt(s_[:, nv:L], ap[:, nv:L] if step == 0 else ap[:, nv:L], (ap[:, nv - sh:L - sh]), ALU.add),
                              reads=[tk_], writes=[ts_])
                        cur = (s_[:, 0:L], ts_, nv)
                        sh *= 2
                    s_ap, ts_, vf = cur
                    assert vf <= 15
                    if n == 0:
                        P.add("dve", o_tt(s_ap[:, 15:31], s_ap[:, 15:31], cst[:, 4 + g * 16:4 + g * 16 + 16], ALU.mult), reads=[ts_, t_cst], writes=[ts_])
                    pb, tpb = TMP()
                    pbb = pb[:, 0:256].bitcast(BF16)
                    P.add("dve", o_stt(pbb, s_ap[:, 15:15 + TW], 1.0 / win, xp_pad[:, g, 16 + n * TW:16 + (n + 1) * TW], ALU.mult, ALU.subtract),
                          reads=[ts_, t_xp[g]], writes=[tpb])
                    ps, tp = PS()
                    P.add("pe", o_mm(ps[:, :], wmisc[:, 8 + g, :], pbb, True, True), reads=[tpb, t_wmisc], writes=[tp])
                    P.add("act", o_act(ys[1][:, g, n * TW:(n + 1) * TW], ps[:, :], AF.Identity, scale=pc(64 + g)), reads=[tp, t_prm], writes=[t_y[1][g]])
            dbg("yb", ys[1], t_y[1])

            for c in range(4):
                for n in range(NT):
                    v_, tv_ = TMP()
                    base = 2 + n * TW
                    P.add("dve", o_ts(v_[:, 0:TW], u_pad[:, c, base:base + TW], pc(68 + c), ALU.mult), reads=[t_u[c], t_prm], writes=[tv_])
                    for k in range(1, 3):
                        P.add("dve", o_stt(v_[:, 0:TW], u_pad[:, c, base + k:base + k + TW], pc(68 + k * 4 + c), v_[:, 0:TW], ALU.mult, ALU.add),
                              reads=[t_u[c], t_prm, tv_], writes=[tv_])
                    P.add("dve", o_tt(ys[2][:, c, n * TW:(n + 1) * TW], v_[:, 0:TW], gb_raw[:, c, n * TW:(n + 1) * TW], ALU.mult),
                          reads=[tv_, t_gb[c]], writes=[t_y[2][c]])
            dbg("yc", ys[2], t_y[2])

            if stop == 'mixabc':
                raise _Stop()
            for j in range(4):
                bi_ = j % 2
                kT, tkT = kT_b[bi_], t_kT[bi_]
                vv, tvv = v_b[bi_], t_v[bi_]
                qq, tqq = q_b[bi_], t_q[bi_]
                P.add("sp", o_dma(kT[:, 0:1024], dap(exd, j * 128 * 1024, [[1024, 128], [1, 1024]])), reads=[t_exa_d[li]], writes=[tkT], dma=f"k{bi_}")
                P.add("sp", o_dma(kT[:, 1024:2048], dap(exs, j * 128 * 1024, [[1024, 128], [1, 1024]])), reads=[t_exa_s[li]], writes=[tkT], dma=f"k{bi_}")
                voff = j * 8 * 128 * 128
                P.add("sp", o_dma(vv[:, 0:8, :], dap(exv_d[li], voff, [[128, 128], [128 * 128, 8], [1, 128]])), reads=[t_exv_d[li]], writes=[tvv], dma=f"v{bi_}")
                P.add("sp", o_dma(vv[:, 8:16, :], dap(exv_s[li], voff, [[128, 128], [128 * 128, 8], [1, 128]])), reads=[t_exv_s[li]], writes=[tvv], dma=f"v{bi_}")

                def ev_q(n, ps, tp, qq=qq, tqq=tqq):
                    sq, tsq = TMP()
                    sqb = sq[:, 0:256].bitcast(BF16)
                    P.add("act", o_act(sqb, ps[:, :], AF.Square), reads=[tp], writes=[tsq])
                    ps2, tp2 = PS()
                    P.add("pe", o_mm(ps2[:, :], bdones, sqb, True, True), reads=[tsq, t_cmb], writes=[tp2])
                    sd, tsd = TMP()
                    P.add("act", o_act(sd[:, 0:TW], ps2[:, :], AF.Sqrt, bias=eps_c, scale=1.0 / 64), reads=[tp2, t_cst], writes=[tsd])
                    P.add("dve", o_recip(sd[:, 0:TW], sd[:, 0:TW]), reads=[tsd], writes=[tsd])
                    P.add("dve", o_stt(qq[:, n * TW:(n + 1) * TW], ps[:, :], pc(80), sd[:, 0:TW], ALU.mult, ALU.mult),
                          reads=[tp, tsd, t_prm], writes=[tqq])
                proj_fm(layer, ("in", 2560 + j * 128), ev_q)

                for e2 in range(2):
                    h = 2 * j + e2
                    R = slice(e2 * 64, e2 * 64 + 64)
                    ct, tct = ct_b[e2], t_ct[e2]
                    for n in range(NT):
                        psc, tpc = PSLO()
                        P.add("pe", o_mm(psc[:, :], self_[:, h, :], qa8[:, n * TW:(n + 1) * TW], True, True), reads=[t_sel, t_qa8], writes=[tpc])
                        P.add("act", o_acopy(ct[:, n * TW:(n + 1) * TW], psc[:, :]), reads=[tpc], writes=[tct])
                    for n in range(NT):
                        pnum, tpn = PSHI()
                        pden, tpd = PSHI()
                        blocks = list(range(8)) + [8 + b for b in range(4 * n + 4)]
                        for bi2, blk in enumerate(blocks):
                            own = blk - 8
                            c0 = 0
                            diag = False
                            if own >= 4 * n:
                                c0 = (own - 4 * n) * 128
                                diag = True
                            ncol = TW - c0
                            pss, tpss = PSLO()
                            P.add("pe", o_mm(pss[:, 0:ncol], kT[R, blk * 128:(blk + 1) * 128], qq[R, n * TW + c0:(n + 1) * TW], True, True),
                                  reads=[tkT, tqq], writes=[tpss])
                            z, tz = TMP()
                            P.add("dve", o_stt(z[:, 0:ncol], pss[:, 0:ncol], kcol[:, blk, h:h + 1], ct[:, n * TW + c0:(n + 1) * TW], ALU.add, ALU.add),
                                  reads=[tpss, t_kcol, tct], writes=[tz])
                            pi = (bi2) % 4
                            pT, tpT = pT_b[pi], t_pT[pi]
                            P.add("act", o_act(pT[:, 0:ncol], z[:, 0:ncol], AF.Exp, scale=0.125), reads=[tz], writes=[tpT])
                            if diag:
                                P.add("dve", o_tt(pT[:, 0:128], pT[:, 0:128], mask_bf, ALU.mult), reads=[tpT, t_cmb], writes=[tpT])
                            first = bi2 == 0
                            last = bi2 == len(blocks) - 1
                            P.add("pe", o_mm(pnum[:, c0:TW], vv[:, blk, :], pT[:, 0:ncol], first, last), reads=[tvv, tpT], writes=[tpn])
                            P.add("pe", o_mm(pden[:, c0:TW], ones_bf, pT[:, 0:ncol], first, last), reads=[t_cmb, tpT], writes=[tpd])
                        rd_, trd = TMP()
                        P.add("dve", o_recip(rd_[R, 0:TW], pden[R, :]), reads=[tpd], writes=[trd])
                        P.add("dve", o_tt(ys[3][R, j, n * TW:(n + 1) * TW], pnum[R, :], rd_[R, 0:TW], ALU.mult), reads=[tpn, trd], writes=[t_y[3][j]])
            dbg("yd", ys[3], t_y[3])

            if stop == 'attn':
                raise _Stop()
            merged = R2
            for j in range(16):
                wb, twb = load_w(layer, ("branch", j))
                accs = [TMP() for _ in range(NT)]
                for k in range(4):
                    wg, twg = load_w(layer, ("in", 4104 + k * 2048 + j * 128))
                    for n in range(NT):
                        psg, tpg = PS()
                        for kc in range(16):
                            P.add("pe", o_mm(psg[:, :], wg[:, kc, :], xn[:, kc, n * TW:(n + 1) * TW], kc == 0, kc == 15), reads=[twg, t_xn[n]], writes=[tpg])
                        psb_, tpb_ = PS()
                        for kc in range(4):
                            P.add("pe", o_mm(psb_[:, :], wb[:, k * 4 + kc, :], ys[k][:, kc, n * TW:(n + 1) * TW], kc == 0, kc == 3),
                                  reads=[twb, t_y[k][kc]], writes=[tpb_])
                        sg, tsg = TMP()
                        P.add("act", o_act(sg[:, 0:TW], psg[:, :], AF.Sigmoid), reads=[tpg], writes=[tsg])
                        acc, tacc = accs[n]
                        if k == 0:
                            P.add("dve", o_tt(acc[:, 0:TW], psb_[:, :], sg[:, 0:TW], ALU.mult), reads=[tpb_, tsg], writes=[tacc])
                        else:
                            P.add("dve", o_tt(sg[:, 0:TW], psb_[:, :], sg[:, 0:TW], ALU.mult), reads=[tpb_, tsg], writes=[tsg])
                            if k < 3:
                                P.add("dve", o_tt(acc[:, 0:TW], acc[:, 0:TW], sg[:, 0:TW], ALU.add), reads=[tacc, tsg], writes=[tacc])
                            else:
                                P.add("dve", o_tt(merged[:, j, n * TW:(n + 1) * TW], acc[:, 0:TW], sg[:, 0:TW], ALU.add), reads=[tacc, tsg], writes=[t_R2[j][n]])
            dbg("merged", merged[:], [t_R2[c][n] for c in range(16) for n in range(NT)])

            if stop == 'merge':
                raise _Stop()
            P.add("sp", o_dma(xT, xsp), reads=[t_xsp], writes=[t_xT[c][n] for c in range(16) for n in range(NT)], dma="unspill")

            for i in range(16):
                wo, two = load_w(layer, ("out", i))
                for n in range(NT):
                    ps, tp = PS()
                    for jc in range(16):
                        P.add("pe", o_mm(ps[:, :], wo[:, jc, :], merged[:, jc, n * TW:(n + 1) * TW], jc == 0, jc == 15), reads=[two, t_R2[jc][n]], writes=[tp])
                    P.add("dve", o_tt(xT[:, i, n * TW:(n + 1) * TW], xT[:, i, n * TW:(n + 1) * TW], ps[:, :], ALU.add), reads=[tp, t_xT[i][n]], writes=[t_xT[i][n]])

            if stop == 'wout':
                raise _Stop()
            rmsnorm_to_xn(li, 16)
            hq = R2
            for q in range(4):
                for hc in range(16):
                    wu, twu = load_w(layer, ("up", q * 16 + hc))
                    for n in range(NT):
                        ps, tp = PS()
                        for kc in range(16):
                            P.add("pe", o_mm(ps[:, :], wu[:, kc, :], xn[:, kc, n * TW:(n + 1) * TW], kc == 0, kc == 15), reads=[twu, t_xn[n]], writes=[tp])
                        r_, tr_ = TMP()
                        P.add("act", o_act(r_[:, 0:TW], ps[:, :], AF.Relu), reads=[tp], writes=[tr_])
                        P.add("dve", o_tt(hq[:, hc, n * TW:(n + 1) * TW], r_[:, 0:TW], r_[:, 0:TW], ALU.mult), reads=[tr_], writes=[t_R2[hc][n]])
                for i in range(16):
                    wd_, twd = load_w(layer, ("down", q, i))
                    for n in range(NT):
                        ps, tp = PS()
                        for hc in range(16):
                            P.add("pe", o_mm(ps[:, :], wd_[:, hc, :], hq[:, hc, n * TW:(n + 1) * TW], hc == 0, hc == 15), reads=[twd, t_R2[hc][n]], writes=[tp])
                        P.add("dve", o_tt(xT[:, i, n * TW:(n + 1) * TW], xT[:, i, n * TW:(n + 1) * TW], ps[:, :], ALU.add), reads=[tp, t_xT[i][n]], writes=[t_xT[i][n]])


        try:
            for li, layer in enumerate(layers):
                layer_body(li, layer)
        except _Stop:
            pass

        for c4 in range(4):
            P.add("sp", o_dma(yout[:, c4 * 4:(c4 + 1) * 4, :], xT[:, c4 * 4:(c4 + 1) * 4, :]),
                  reads=[t_xT[c][n] for c in range(c4 * 4, c4 * 4 + 4) for n in range(NT)], writes=[t_yout], dma="out")
        fw = ["out"]
        for name in dbg_list:
            fw.append("dbg_" + name)
        P.emit(final_waits=fw)
    return nc


_CACHE = {}


def make_in_maps(inp, x_override=None):
    W = pack_weights(inp)
    prm = pack_params(inp)
    cm, sel = shared_consts()
    x = np.asarray(inp["x"], np.float32) if x_override is None else x_override
    in_maps = []
    for core in range(8):
        b, half = core // 2, core % 2
        xs = x[b, half * T:(half + 1) * T, :]
        xTc = np.ascontiguousarray(xs.T.reshape(16, 128, T).transpose(1, 0, 2))
        in_maps.append({"xT": xTc, "W": W.reshape(DEPTH * NCH, 128, 16, 128), "prm": prm,
                        "cst": core_consts(half), "cmat": cm, "sel": sel})
    return in_maps


def gather_out(results, key="yT"):
    out = np.zeros((4, 2 * T, D), np.float32)
    for core in range(8):
        b, half = core // 2, core % 2
        yT = np.asarray(results[core][key])
        out[b, half * T:(half + 1) * T, :] = yT.transpose(1, 0, 2).reshape(D, T).T
    return out


def kernel(**inputs):
    inp = {k: np.asarray(v) for k, v in inputs.items()}
    if "nc" not in _CACHE:
        _CACHE["nc"] = build(layers=(0, 1))
    nc = _CACHE["nc"]
    in_maps = make_in_maps(inp)
    res = run_bass_kernel_spmd(nc, in_maps, core_ids=list(range(8)))
    return gather_out(res.results)
```

```python
import contextlib
import numpy as np
import ml_dtypes
import concourse.bass as bass
import concourse.mybir as mybir
from concourse.bass_utils import run_bass_kernel_spmd

F32 = mybir.dt.float32
BF16 = mybir.dt.bfloat16
AF = mybir.ActivationFunctionType
ALU = mybir.AluOpType

D = 2048
T = 1024
TW = 512
NT = 2
DEPTH = 2
NPRM = 96
NCST = 72
EPS = 1e-6


class Tok:
    __slots__ = ("name", "grp", "lo", "hi", "writer", "readers")

    def __init__(self, name, grp=None, lo=0, hi=0):
        self.name = name
        self.grp = grp
        self.lo = lo
        self.hi = hi
        self.writer = None
        self.readers = {}


class Op:
    __slots__ = ("eng", "idx", "fn", "deps", "signal", "val", "dma_sem", "dma_val", "waits", "inc", "tag")


class Prog:
    ENGS = ("pe", "act", "dve", "pool", "sp")

    def __init__(self, nc):
        self.nc = nc
        self.ops = {e: [] for e in self.ENGS}
        self.groups = {}
        self.dma_count = {}
        self.sync_same = {"act": True, "dve": True, "pool": True, "pe": False, "sp": False}

    def tok(self, name, grp=None, lo=0, hi=0):
        t = Tok(name, grp, lo, hi)
        if grp is not None:
            self.groups.setdefault(grp, []).append(t)
        return t

    def _overl(self, t):
        if t.grp is None:
            return (t,)
        return [u for u in self.groups[t.grp] if u.lo < t.hi and t.lo < u.hi]

    def add(self, eng, fn, reads=(), writes=(), dma=None, inc=16):
        op = Op()
        op.eng = eng
        op.idx = len(self.ops[eng])
        op.fn = fn
        op.signal = False
        op.val = None
        op.dma_sem = dma
        op.dma_val = None
        op.waits = None
        op.inc = inc
        import sys as _sys
        fr = _sys._getframe(1)
        op.tag = (fr.f_lineno, fr.f_back.f_lineno if fr.f_back else 0)
        deps = set()
        for r in reads:
            for u in self._overl(r):
                if u.writer is not None:
                    deps.add(u.writer)
        for w in writes:
            for u in self._overl(w):
                if u.writer is not None:
                    deps.add(u.writer)
                for o in u.readers.values():
                    deps.add(o)
        op.deps = deps
        if dma is not None:
            self.dma_count[dma] = self.dma_count.get(dma, 0) + inc
            op.dma_val = self.dma_count[dma]
        for r in reads:
            key = eng if dma is None else ("dma", dma)
            r.readers[key] = op
        for w in writes:
            w.writer = op
            w.readers = {}
        self.ops[eng].append(op)
        return op

    def finalize(self):
        for e in self.ENGS:
            seen_eng = {}
            seen_dma = {}
            for op in self.ops[e]:
                best_e = {}
                best_d = {}
                for d in op.deps:
                    if d.dma_sem is not None:
                        if seen_dma.get(d.dma_sem, 0) >= d.dma_val:
                            continue
                        if best_d.get(d.dma_sem) is None or best_d[d.dma_sem].dma_val < d.dma_val:
                            best_d[d.dma_sem] = d
                    else:
                        if d.eng == e and (not self.sync_same[e] or d.idx >= op.idx):
                            continue
                        if seen_eng.get(d.eng, -1) >= d.idx:
                            continue
                        if best_e.get(d.eng) is None or best_e[d.eng].idx < d.idx:
                            best_e[d.eng] = d
                for k, d in best_e.items():
                    seen_eng[k] = d.idx
                    d.signal = True
                for k, d in best_d.items():
                    seen_dma[k] = d.dma_val
                op.waits = list(best_e.values()) + list(best_d.values())
        for e in self.ENGS:
            c = 0
            for op in self.ops[e]:
                if op.dma_sem is None and op.signal:
                    c += 1
                    op.val = c

    def emit(self, final_waits=()):
        nc = self.nc
        self.finalize()
        with contextlib.ExitStack() as st:
            esem = {e: st.enter_context(nc.semaphore("s_" + e)) for e in self.ENGS}
            dsem = {n: st.enter_context(nc.semaphore("d_" + n)) for n in self.dma_count}
            block = st.enter_context(nc.Block())
            binder = {"pe": block.tensor, "act": block.scalar, "dve": block.vector,
                      "pool": block.gpsimd, "sp": block.sync}

            def run(e, eng):
                for op in self.ops[e]:
                    for d in op.waits:
                        if d.dma_sem is not None:
                            eng.wait_ge(dsem[d.dma_sem], d.dma_val)
                        else:
                            eng.wait_ge(esem[d.eng], d.val)
                    ins = op.fn(eng)
                    import os as _os
                    if _os.environ.get("DBG_INS") and getattr(getattr(ins, "ins", None), "name", None) == _os.environ["DBG_INS"]:
                        print("DBG_INS", op.eng, op.idx, op.tag)
                    if op.dma_sem is not None:
                        ins.then_inc(dsem[op.dma_sem], op.inc)
                    elif op.signal:
                        ins.then_inc(esem[e], 1)
                if e == "sp":
                    for n in final_waits:
                        eng.wait_ge(dsem[n], self.dma_count[n])

            for e in self.ENGS:
                def mk(e):
                    def body(eng):
                        run(e, eng)
                    return body
                binder[e](mk(e))


def o_mm(out, lhsT, rhs, start, stop):
    return lambda e: e.matmul(out, lhsT, rhs, start=start, stop=stop)


def o_tr(out, in_, ident):
    return lambda e: e.matmul(out, in_, ident, start=True, stop=True, is_transpose=True)


def o_act(out, in_, func, bias=None, scale=None):
    def f(e):
        kw = {}
        if bias is not None:
            kw["bias"] = bias
        if scale is not None:
            kw["scale"] = scale
        return e.activation(out, in_, func, **kw)
    return f


def o_tt(out, a, b, op):
    return lambda e: e.tensor_tensor(out=out, in0=a, in1=b, op=op)


def o_ts(out, a, s1, op0, s2=None, op1=None):
    if op1 is None:
        return lambda e: e.tensor_scalar(out=out, in0=a, scalar1=s1, scalar2=None, op0=op0)
    return lambda e: e.tensor_scalar(out=out, in0=a, scalar1=s1, scalar2=s2, op0=op0, op1=op1)


def o_stt(out, in0, scalar, in1, op0, op1):
    return lambda e: e.scalar_tensor_tensor(out=out, in0=in0, scalar=scalar, in1=in1, op0=op0, op1=op1)


def o_scan(out, d0, d1, init, op0, op1):
    return lambda e: e.tensor_tensor_scan(out=out, data0=d0, data1=d1, initial=init, op0=op0, op1=op1)


def o_vcopy(out, in_):
    return lambda e: e.tensor_copy(out=out, in_=in_)


def o_acopy(out, in_):
    return lambda e: e.copy(out, in_)


def o_recip(out, in_):
    return lambda e: e.reciprocal(out=out, in_=in_)


def o_memset(ap, v):
    return lambda e: e.memset(ap, v)


def o_dma(out, in_):
    return lambda e: e.dma_start(out=out, in_=in_)


def chunk_order():
    specs = [("misc",)]
    for c in range(4):
        specs.append(("in", 0 + c * 128))
    for c in range(4):
        specs.append(("in", 512 + c * 128))
    for c in range(4):
        specs.append(("in", 1024 + 512 + c * 128))
        specs.append(("in", 1024 + 1024 + c * 128))
        specs.append(("in", 1024 + c * 128))
    for c in range(4):
        specs.append(("in", 2560 + 512 + c * 128))
    for c in range(4):
        specs.append(("in", 2560 + 1024 + c * 128))
    for c in range(4):
        specs.append(("in", 2560 + c * 128))
    for j in range(16):
        specs.append(("branch", j))
        for k in range(4):
            specs.append(("in", 4104 + k * 2048 + j * 128))
    for i in range(16):
        specs.append(("out", i))
    for q in range(4):
        for hc in range(16):
            specs.append(("up", q * 16 + hc))
        for i in range(16):
            specs.append(("down", q, i))
    return specs


NCH = len(chunk_order())


def pack_weights(inp):
    specs = chunk_order()
    W = np.zeros((DEPTH * NCH, 128, 16, 128), np.float32)

    def std(mat):
        return mat.reshape(16, 128, 128).transpose(1, 0, 2)

    for l in range(DEPTH):
        w_in = inp["w_in"][l]
        for n, s in enumerate(specs):
            dst = W[l * NCH + n]
            if s[0] == "misc":
                for ri, name in enumerate(("lru_wr", "lru_wi")):
                    wl = inp[name][l]
                    for c in range(4):
                        dst[0:64, ri * 4 + c, 0:64] = wl[2 * c]
                        dst[64:128, ri * 4 + c, 64:128] = wl[2 * c + 1]
                for g in range(4):
                    dst[:, 8 + g, :] = inp["pool_w"][l][g]
                wf = w_in[:, 4096:4104]
                dst[:, 12, :] = wf.reshape(16, 128, 8).transpose(1, 0, 2).reshape(128, 128)
            elif s[0] == "in":
                dst[:] = std(w_in[:, s[1]:s[1] + 128])
            elif s[0] == "branch":
                j = s[1]
                wb = inp["w_branch"][l][:, :, j * 128:(j + 1) * 128]
                dst[:] = wb.reshape(4, 4, 128, 128).transpose(2, 0, 1, 3).reshape(128, 16, 128)
            elif s[0] == "out":
                i = s[1]
                dst[:] = std(inp["w_out"][l][:, i * 128:(i + 1) * 128])
            elif s[0] == "up":
                j = s[1]
                dst[:] = std(inp["w_mlp_up"][l][:, j * 128:(j + 1) * 128])
            elif s[0] == "down":
                q, i = s[1], s[2]
                dst[:] = std(inp["w_mlp_down"][l][q * 2048:(q + 1) * 2048, i * 128:(i + 1) * 128])
    return W.reshape(DEPTH * NCH, 128, 2048)


def pack_params(inp):
    prm = np.zeros((DEPTH, 128, NPRM), np.float32)

    def col(v):
        return v.reshape(-1, 128).T

    for l in range(DEPTH):
        p = prm[l]
        p[:, 0:16] = col(inp["norm_mix_g"][l])
        p[:, 16:32] = col(inp["norm_mlp_g"][l])
        for k in range(4):
            p[:, 32 + k * 4:36 + k * 4] = col(inp["lru_conv_w"][l][k])
        p[:, 48:52] = col(inp["lru_conv_b"][l])
        p[:, 52:56] = col(inp["lru_br"][l])
        p[:, 56:60] = col(inp["lru_bi"][l])
        p[:, 60:64] = col(inp["lru_lambda"][l])
        p[:, 64:68] = col(inp["pool_scale"][l])
        for k in range(3):
            p[:, 68 + k * 4:72 + k * 4] = col(inp["sconv_w"][l][k])
        p[:, 80] = np.tile(inp["q_norm_g"][l], 2)
        p[:, 81] = np.tile(inp["k_norm_g"][l], 2)
        p[0:8, 82] = inp["forget_b"][l]
    return prm


def core_consts(half):
    cst = np.zeros((128, NCST), np.float32)
    cst[:, 0] = float(half)
    cst[:, 1] = 0.0 if half else -8.0e4
    cst[:, 2] = EPS
    cst[:, 3] = 1.0
    for g, win in enumerate((2, 4, 8, 16)):
        for t in range(16):
            cst[:, 4 + g * 16 + t] = 1.0 if half else float(win) / float(min(t + 1, win))
    return cst


def shared_consts():
    cm = np.zeros((128, 3, 128), np.float32)
    s = np.arange(128)[:, None]
    t = np.arange(128)[None, :]
    cm[:, 0, :] = (s <= t).astype(np.float32)
    cm[0:64, 1, 0:64] = 1.0
    cm[64:128, 1, 64:128] = 1.0
    cm[:, 2, :] = np.eye(128, dtype=np.float32)
    sel = np.zeros((8, 8, 128), np.float32)
    for h in range(8):
        sel[h, h, :] = 1.0
    return cm, sel


class _Stop(Exception):
    pass


def build(layers=(0, 1), debug=(), stop=None, ncores=8):
    nc = bass.Bass("TRN2", target_bir_lowering=False)
    nL = len(layers)
    xin = nc.dram_tensor("xT", [128, 16, T], F32, kind="ExternalInput").ap()
    Wd = nc.dram_tensor("W", [nL * NCH, 128, 16, 128], F32, kind="ExternalInput").ap()
    prmd = nc.dram_tensor("prm", [DEPTH, 128, NPRM], F32, kind="ExternalInput").ap()
    cstd = nc.dram_tensor("cst", [128, NCST], F32, kind="ExternalInput").ap()
    cmd = nc.dram_tensor("cmat", [128, 3, 128], F32, kind="ExternalInput").ap()
    seld = nc.dram_tensor("sel", [8, 8, 128], F32, kind="ExternalInput").ap()
    yout = nc.dram_tensor("yT", [128, 16, T], F32, kind="ExternalOutput").ap()
    xsp = nc.dram_tensor("xspill", [128, 16, T], F32, kind="Internal").ap()
    EXA = 522
    exa_s = [nc.dram_tensor(f"exa_s{l}", [EXA, 1024], BF16, kind="Internal") for l in range(nL)]
    exa_d = [nc.dram_tensor(f"exa_d{l}", [2 * EXA, 1024], BF16, kind="Internal") for l in range(nL)]
    exv_s = [nc.dram_tensor(f"exv_s{l}", [512, 1024], BF16, kind="Internal") for l in range(nL)]
    exv_d = [nc.dram_tensor(f"exv_d{l}", [1024, 1024], BF16, kind="Internal") for l in range(nL)]
    exb_s = [nc.dram_tensor(f"exb_s{l}", [17, 512], F32, kind="Internal") for l in range(nL)]
    exb_d = [nc.dram_tensor(f"exb_d{l}", [34, 512], F32, kind="Internal") for l in range(nL)]
    dbg_out = {}
    for name, shape, dt in debug:
        dbg_out[name] = nc.dram_tensor("dbg_" + name, list(shape), dt, kind="ExternalOutput").ap()
    groups = [[2 * g, 2 * g + 1] for g in range(ncores // 2)]

    def dap(th, off, dims):
        return bass.AP(th, off, [list(d) for d in dims])

    st = contextlib.ExitStack()
    with st:
        P = Prog(nc)

        def sb(name, shape, dt):
            return st.enter_context(nc.sbuf_tensor(name, list(shape), dt))

        R0 = sb("R0", [128, 16 * T + 64], F32)
        xn = sb("xn", [128, 16, T], BF16)
        R2 = sb("R2", [128, 16, T], BF16)
        NSLOT = 4
        wsl = [sb(f"w{i}", [128, 16, 128], BF16) for i in range(NSLOT)]
        wbr = [sb(f"wbr{i}", [128, 16, 128], BF16) for i in range(2)]
        NTMP = 16
        tmpf = [sb(f"tmp{i}", [128, 528], F32) for i in range(NTMP)]
        prm = sb("prm_sb", [128, nL, NPRM], F32)
        cst = sb("cst_sb", [128, NCST], F32)
        cmf = sb("cmf", [128, 3, 128], F32)
        cmb = sb("cmb", [128, 3, 128], BF16)
        self_ = sb("sel_sb", [8, 8, 128], F32)
        qa8 = sb("qa8", [8, T], F32)
        kcol = sb("kcol", [128, 16, 8], F32)
        small = sb("small", [128, 64], F32)
        tls = sb("tls", [128, 4, 20], BF16)
        tlr = sb("tlr", [128, 4, 20], BF16)

        R0b = R0[:].bitcast(BF16)
        xT = R0[:, 0:16 * T].rearrange("p (c t) -> p c t", c=16)
        ys = [R0b[:, k * 4096:(k + 1) * 4096].rearrange("p (c t) -> p c t", c=4) for k in range(4)]
        MB = 16384
        xa_pad = R0b[:, MB:MB + 4 * 1028].rearrange("p (c t) -> p c t", c=4)
        xp_pad = R0b[:, MB + 4112:MB + 4112 + 4 * 1040].rearrange("p (c t) -> p c t", c=4)
        u_pad = R0b[:, MB + 8272:MB + 8272 + 4 * 1028].rearrange("p (c t) -> p c t", c=4)
        gb_raw = R0b[:, MB + 12384:MB + 12384 + 4096].rearrange("p (c t) -> p c t", c=4)
        AB = MB
        kT_b = [R0b[:, AB + i * 2048:AB + (i + 1) * 2048] for i in range(2)]
        v_b = [R0b[:, AB + 4096 + i * 2048:AB + 4096 + (i + 1) * 2048].rearrange("p (b c) -> p b c", b=16) for i in range(2)]
        q_b = [R0b[:, AB + 8192 + i * 1024:AB + 8192 + (i + 1) * 1024] for i in range(2)]
        ct_b = [R0[:, 8192 + 5120 + i * 1024:8192 + 5120 + (i + 1) * 1024] for i in range(2)]
        pT_b = [R0b[:, AB + 14336 + i * 512:AB + 14336 + (i + 1) * 512] for i in range(4)]

        G = "R0"

        def tk(name, lo, hi):
            return P.tok(name, G, lo, hi)

        t_xT = [[tk(f"xT{c}_{n}", (c * T + n * TW) * 4, (c * T + n * TW + TW) * 4) for n in range(NT)] for c in range(16)]
        t_y = [[tk(f"y{k}_{c}", (k * 4096 + c * 1024) * 2, (k * 4096 + c * 1024 + 1024) * 2) for c in range(4)] for k in range(4)]
        b0 = MB * 2
        t_xa = [tk(f"xa{c}", b0 + c * 1028 * 2, b0 + (c + 1) * 1028 * 2) for c in range(4)]
        t_xp = [tk(f"xp{c}", b0 + (4112 + c * 1040) * 2, b0 + (4112 + (c + 1) * 1040) * 2) for c in range(4)]
        t_u = [tk(f"u{c}", b0 + (8272 + c * 1028) * 2, b0 + (8272 + (c + 1) * 1028) * 2) for c in range(4)]
        t_gb = [tk(f"gb{c}", b0 + (12384 + c * 1024) * 2, b0 + (12384 + (c + 1) * 1024) * 2) for c in range(4)]
        t_kT = [tk(f"kT{i}", b0 + i * 4096, b0 + (i + 1) * 4096) for i in range(2)]
        t_v = [tk(f"v{i}", b0 + 8192 + i * 4096, b0 + 8192 + (i + 1) * 4096) for i in range(2)]
        t_q = [tk(f"q{i}", b0 + 16384 + i * 2048, b0 + 16384 + (i + 1) * 2048) for i in range(2)]
        t_ct = [tk(f"ct{i}", (8192 + 5120 + i * 1024) * 4, (8192 + 5120 + (i + 1) * 1024) * 4) for i in range(2)]
        t_pT = [tk(f"pT{i}", b0 + (14336 + i * 512) * 2, b0 + (14336 + (i + 1) * 512) * 2) for i in range(4)]
        t_xn = [P.tok(f"xn{n}") for n in range(NT)]
        t_R2 = [[P.tok(f"R2_{c}_{n}") for n in range(NT)] for c in range(16)]
        t_w = [P.tok(f"w{i}") for i in range(NSLOT)]
        t_wbr = [P.tok(f"wbr{i}") for i in range(2)]
        t_tmp = [P.tok(f"tmp{i}") for i in range(NTMP)]
        t_prm, t_cst, t_cm, t_cmb, t_sel = P.tok("prm"), P.tok("cst"), P.tok("cm"), P.tok("cmb"), P.tok("sel")
        t_qa8, t_kcol, t_small, t_tls, t_tlr = P.tok("qa8"), P.tok("kcol"), P.tok("small"), P.tok("tls"), P.tok("tlr")
        t_xsp = P.tok("xsp")
        t_yout = P.tok("yout")
        t_exa_s = [P.tok(f"exa_s{l}") for l in range(nL)]
        t_exa_d = [P.tok(f"exa_d{l}") for l in range(nL)]
        t_exb_s = [P.tok(f"exb_s{l}") for l in range(nL)]
        t_exv_s = [P.tok(f"exv_s{l}") for l in range(nL)]
        t_exv_d = [P.tok(f"exv_d{l}") for l in range(nL)]
        t_exb_d = [P.tok(f"exb_d{l}") for l in range(nL)]

        psb = [st.enter_context(nc.psum_tensor(f"ps{i}", [128, 512], F32)) for i in range(8)]
        t_ps = [P.tok(f"ps{i}") for i in range(8)]
        ps_ctr = [0]

        def PS():
            i = ps_ctr[0] % 8
            ps_ctr[0] += 1
            return psb[i], t_ps[i]

        lo_ctr = [0]

        def PSLO():
            i = lo_ctr[0] % 4
            lo_ctr[0] += 1
            return psb[i], t_ps[i]

        hi_ctr = [0]

        def PSHI():
            i = 4 + (hi_ctr[0] % 4)
            hi_ctr[0] += 1
            return psb[i], t_ps[i]

        tmp_ctr = [0]

        def TMP():
            i = tmp_ctr[0] % NTMP
            tmp_ctr[0] += 1
            return tmpf[i], t_tmp[i]

        w_ctr = [0]
        specs = chunk_order()

        slot_ctr = [0]
        br_ctr = [0]

        def load_w(layer, expect):
            n = w_ctr[0]
            li = n // NCH
            assert layers[li] == layer and specs[n % NCH][0] == expect[0] and tuple(specs[n % NCH][1:]) == tuple(expect[1:]), (n, specs[n % NCH], expect)
            w_ctr[0] += 1
            if expect[0] == "branch":
                s = br_ctr[0] % 2
                br_ctr[0] += 1
                P.add("pool", o_dma(wbr[s][:], Wd[li * NCH + (n % NCH)]), writes=[t_wbr[s]], dma=f"wbr{s}")
                return wbr[s], t_wbr[s]
            s = slot_ctr[0] % NSLOT
            slot_ctr[0] += 1
            P.add("pool", o_dma(wsl[s][:], Wd[li * NCH + (n % NCH)]), writes=[t_w[s]], dma=f"w{s}")
            return wsl[s], t_w[s]

        P.add("sp", o_dma(prm[:], prmd[layers[0]:layers[0] + nL].rearrange("l p n -> p l n")), writes=[t_prm], dma="c0")
        P.add("sp", o_dma(cst[:], cstd), writes=[t_cst], dma="c1")
        P.add("sp", o_dma(cmf[:], cmd), writes=[t_cm], dma="c2")
        P.add("sp", o_dma(self_[:], seld), writes=[t_sel], dma="c3")
        P.add("dve", o_vcopy(cmb[:, 0:2, :], cmf[:, 0:2, :]), reads=[t_cm], writes=[t_cmb])
        P.add("dve", o_memset(cmb[:, 2, :], 1.0), writes=[t_cmb])
        mask_bf = cmb[:, 0, :]
        bdones = cmb[:, 1, :]
        ones_bf = cmb[:, 2, :]
        ident = cmf[:, 2, :]
        flag = cst[:, 0:1]
        negbig8 = cst[:, 1:2]
        eps_c = cst[:, 2:3]
        one_c = cst[:, 3:4]

        for c4 in range(4):
            P.add("sp", o_dma(xT[:, c4 * 4:(c4 + 1) * 4, :], xin[:, c4 * 4:(c4 + 1) * 4, :]),
                  writes=[t_xT[c][n] for c in range(c4 * 4, c4 * 4 + 4) for n in range(NT)], dma=f"x{c4}")

        def rmsnorm_to_xn(li, gcol0):
            for n in range(NT):
                ps, tp = PS()
                for c in range(16):
                    sq, tsq = TMP()
                    sqb = sq[:, 0:256].bitcast(BF16)
                    P.add("act", o_act(sqb, xT[:, c, n * TW:(n + 1) * TW], AF.Square), reads=[t_xT[c][n]], writes=[tsq])
                    P.add("pe", o_mm(ps[:, :], ones_bf, sqb, c == 0, c == 15), reads=[tsq, t_cmb], writes=[tp])
                sd, tsd = TMP()
                P.add("act", o_act(sd[:, 0:TW], ps[:, :], AF.Sqrt, bias=eps_c, scale=1.0 / D), reads=[tp, t_cst], writes=[tsd])
                rs, trs = TMP()
                P.add("dve", o_recip(rs[:, 0:TW], sd[:, 0:TW]), reads=[tsd], writes=[trs])
                for c in range(16):
                    P.add("dve", o_stt(xn[:, c, n * TW:(n + 1) * TW], xT[:, c, n * TW:(n + 1) * TW],
                                       prm[:, li, gcol0 + c:gcol0 + c + 1], rs[:, 0:TW], ALU.mult, ALU.mult),
                          reads=[t_xT[c][n], trs, t_prm], writes=[t_xn[n]])

        def proj_fm(layer, expect, evac):
            w, tw = load_w(layer, expect)
            for n in range(NT):
                ps, tp = PS()
                for kc in range(16):
                    P.add("pe", o_mm(ps[:, :], w[:, kc, :], xn[:, kc, n * TW:(n + 1) * TW], kc == 0, kc == 15),
                          reads=[tw, t_xn[n]], writes=[tp])
                evac(n, ps, tp)

        dbg_list = []

        def dbg(name, ap, toks):
            if name in dbg_out:
                dbg_list.append(name)
                P.add("sp", o_dma(dbg_out[name], ap), reads=list(toks), writes=[P.tok("dbg_" + name)], dma="dbg_" + name)

        def layer_body(li, layer):
            pl = prm[:, li, :]

            def pc(i):
                return prm[:, li, i:i + 1]

            rmsnorm_to_xn(li, 0)
            P.add("sp", o_dma(xsp, xT), reads=[t_xT[c][n] for c in range(16) for n in range(NT)], writes=[t_xsp], dma="spill")

            if stop == 'norm':
                raise _Stop()
            wm, twm = load_w(layer, ("misc",))
            wmisc = R2[:, 0:2, :].rearrange("p a t -> p (a t)").rearrange("p (s c) -> p s c", s=16)
            t_wmisc = t_R2[0][0]
            t_wm_all = [t_R2[0][0], t_R2[0][1], t_R2[1][0], t_R2[1][1]]
            P.add("dve", o_vcopy(wmisc, wm[:]), reads=[twm], writes=t_wm_all)
            P.add("act", o_act(small[:, 0:4], pl[:, 60:64], AF.Exp, scale=-1.0), reads=[t_prm], writes=[t_small])
            P.add("act", o_act(small[:, 0:4], small[:, 0:4], AF.Ln, bias=one_c), reads=[t_small, t_cst], writes=[t_small])
            P.add("dve", o_ts(small[:, 0:4], small[:, 0:4], -8.0, ALU.mult), reads=[t_small], writes=[t_small])
            P.add("dve", o_ts(small[:, 4:5], pl[:, 82:83], -1.0, ALU.mult), reads=[t_prm, t_small], writes=[t_small])
            P.add("dve", o_memset(xa_pad[:, :, 0:4], 0.0), writes=t_xa)
            P.add("dve", o_memset(xp_pad[:, :, 0:16], 0.0), writes=t_xp)
            P.add("dve", o_memset(u_pad[:, :, 0:4], 0.0), writes=t_u)

            def lru_parts(c, n, final, state):
                loc = {}

                def p1():
                    u, tu = TMP()
                    base = 1 + n * TW
                    P.add("dve", o_ts(u[:, 0:TW], xa_pad[:, c, base:base + TW], pc(32 + c), ALU.mult, pc(48 + c), ALU.add),
                          reads=[t_xa[c], t_prm], writes=[tu])
                    for k in range(1, 4):
                        P.add("dve", o_stt(u[:, 0:TW], xa_pad[:, c, base + k:base + k + TW], pc(32 + k * 4 + c), u[:, 0:TW], ALU.mult, ALU.add),
                              reads=[t_xa[c], t_prm, tu], writes=[tu])
                    ub, tub = TMP()
                    ubb = ub[:, 0:256].bitcast(BF16)
                    P.add("act", o_acopy(ubb, u[:, 0:TW]), reads=[tu], writes=[tub])
                    loc.update(u=u, tu=tu, ubb=ubb, tub=tub)

                def p2():
                    u, tu, ubb, tub = loc["u"], loc["tu"], loc["ubb"], loc["tub"]
                    psr, tpr = PS()
                    P.add("pe", o_mm(psr[:, :], wmisc[:, c, :], ubb, True, True), reads=[tub, t_wmisc], writes=[tpr])
                    psi, tpi = PS()
                    P.add("pe", o_mm(psi[:, :], wmisc[:, 4 + c, :], ubb, True, True), reads=[tub, t_wmisc], writes=[tpi])
                    r, tr_ = TMP()
                    P.add("act", o_act(r[:, 0:TW], psr[:, :], AF.Sigmoid, bias=pc(52 + c)), reads=[tpr, t_prm], writes=[tr_])
                    gi, tgi = TMP()
                    P.add("act", o_act(gi[:, 0:TW], psi[:, :], AF.Sigmoid, bias=pc(56 + c)), reads=[tpi, t_prm], writes=[tgi])
                    P.add("act", o_act(r[:, 0:TW], r[:, 0:TW], AF.Exp, scale=small[:, c:c + 1]), reads=[tr_, t_small], writes=[tr_])
                    sq, tsq = TMP()
                    P.add("dve", o_tt(sq[:, 0:TW], r[:, 0:TW], r[:, 0:TW], ALU.mult), reads=[tr_], writes=[tsq])
                    P.add("act", o_act(sq[:, 0:TW], sq[:, 0:TW], AF.Sqrt, bias=one_c, scale=-1.0), reads=[tsq, t_cst], writes=[tsq])
                    P.add("dve", o_tt(gi[:, 0:TW], gi[:, 0:TW], u[:, 0:TW], ALU.mult), reads=[tgi, tu], writes=[tgi])
                    P.add("dve", o_tt(gi[:, 0:TW], gi[:, 0:TW], sq[:, 0:TW], ALU.mult), reads=[tgi, tsq], writes=[tgi])
                    h, th = TMP()
                    if n == 0:
                        init = small[:, 8 + c:9 + c] if final else 0.0
                        rd = [t_small] if final else []
                    else:
                        init = state["prev_h"][0][:, TW - 1:TW]
                        rd = [state["prev_h"][1]]
                    P.add("dve", o_scan(h[:, 0:TW], r[:, 0:TW], gi[:, 0:TW], init, ALU.mult, ALU.add),
                          reads=[tr_, tgi] + rd, writes=[th])
                    state["prev_h"] = (h, th)
                    if final:
                        P.add("act", o_acopy(ys[0][:, c, n * TW:(n + 1) * TW], h[:, 0:TW]), reads=[th], writes=[t_y[0][c]])
                    elif n == NT - 1:
                        P.add("act", o_acopy(small[:, 12 + c:13 + c], h[:, TW - 1:TW]), reads=[th], writes=[t_small])
                return p1, p2

            def lru_chunk(c, final):
                stt_ = {}
                for n in range(NT):
                    p1, p2 = lru_parts(c, n, final, stt_)
                    p1()
                    p2()

            pend = []
            wait2 = [None]

            def tick():
                if wait2[0] is not None:
                    wait2[0]()
                    wait2[0] = None
                if pend:
                    p1, p2 = pend.pop(0)
                    p1()
                    wait2[0] = p2

            for c in range(4):
                def ev(n, ps, tp, c=c):
                    P.add("act", o_acopy(xa_pad[:, c, 4 + n * TW:4 + (n + 1) * TW], ps[:, :]), reads=[tp], writes=[t_xa[c]])
                proj_fm(layer, ("in", c * 128), ev)
                stt_c = {}
                for n in range(NT):
                    pend.append(lru_parts(c, n, False, stt_c))
            for c in range(4):
                def ev(n, ps, tp, c=c):
                    P.add("act", o_acopy(xp_pad[:, c, 16 + n * TW:16 + (n + 1) * TW], ps[:, :]), reads=[tp], writes=[t_xp[c]])
                proj_fm(layer, ("in", 512 + c * 128), ev)
                tick()
            for c in range(4):
                gcs = []

                def ev_gc(n, ps, tp):
                    g_, tg_ = TMP()
                    P.add("act", o_acopy(g_[:, 0:TW], ps[:, :]), reads=[tp], writes=[tg_])
                    gcs.append((g_, tg_))
                proj_fm(layer, ("in", 1536 + c * 128), ev_gc)

                def ev_xc(n, ps, tp, c=c):
                    g_, tg_ = gcs[n]
                    P.add("dve", o_tt(u_pad[:, c, 4 + n * TW:4 + (n + 1) * TW], ps[:, :], g_[:, 0:TW], ALU.mult),
                          reads=[tp, tg_], writes=[t_u[c]])
                proj_fm(layer, ("in", 2048 + c * 128), ev_xc)
                tick()

                def ev_gb(n, ps, tp, c=c):
                    P.add("act", o_acopy(gb_raw[:, c, n * TW:(n + 1) * TW], ps[:, :]), reads=[tp], writes=[t_gb[c]])
                proj_fm(layer, ("in", 1024 + c * 128), ev_gb)
                tick()

            if stop == 'projabc':
                raise _Stop()
            while pend or wait2[0] is not None:
                tick()
            exs, exd = exa_s[li], exa_d[li]
            for c in range(4):
                kst, tkst = TMP()
                kstb = kst[:, 0:512].bitcast(BF16)

                def ev_k(n, ps, tp, c=c, kstb=kstb, tkst=tkst):
                    sq, tsq = TMP()
                    sqb = sq[:, 0:256].bitcast(BF16)
                    P.add("act", o_act(sqb, ps[:, :], AF.Square), reads=[tp], writes=[tsq])
                    ps2, tp2 = PS()
                    P.add("pe", o_mm(ps2[:, :], bdones, sqb, True, True), reads=[tsq, t_cmb], writes=[tp2])
                    sd, tsd = TMP()
                    P.add("act", o_act(sd[:, 0:TW], ps2[:, :], AF.Sqrt, bias=eps_c, scale=1.0 / 64), reads=[tp2, t_cst], writes=[tsd])
                    P.add("dve", o_recip(sd[:, 0:TW], sd[:, 0:TW]), reads=[tsd], writes=[tsd])
                    P.add("dve", o_stt(kstb[:, n * TW:(n + 1) * TW], ps[:, :], pc(81), sd[:, 0:TW], ALU.mult, ALU.mult),
                          reads=[tp, tsd, t_prm], writes=[tkst])
                proj_fm(layer, ("in", 2560 + 512 + c * 128), ev_k)
                P.add("sp", o_dma(dap(exs, c * 128 * 1024, [[1024, 128], [1, 1024]]), kstb), reads=[tkst], writes=[t_exa_s[li]], dma="exsta")

            for c in range(4):
                w, tw = load_w(layer, ("in", 2560 + 1024 + c * 128))
                vst, tvst = TMP()
                vstb = vst[:, 0:512].bitcast(BF16).rearrange("p (b c) -> p b c", b=8)
                for half in range(2):
                    ps, tp = PS()
                    for tb in range(4):
                        blk = half * 4 + tb
                        n = blk // 4
                        for kc in range(16):
                            P.add("pe", o_mm(ps[:, tb * 128:(tb + 1) * 128], xn[:, kc, blk * 128:(blk + 1) * 128], w[:, kc, :], kc == 0, kc == 15),
                                  reads=[tw, t_xn[n]], writes=[tp])
                    P.add("act", o_acopy(vstb[:, half * 4:(half + 1) * 4, :], ps[:, :].rearrange("p (b c) -> p b c", b=4)), reads=[tp], writes=[tvst])
                P.add("sp", o_dma(dap(exv_s[li], c * 8 * 128 * 128, [[128, 128], [128 * 128, 8], [1, 128]]), vstb),
                      reads=[tvst], writes=[t_exv_s[li]], dma="exstv")

            prev_S = None
            psk, tpk = PS()
            for n in range(NT):
                ps, tp = PS()
                wf = wmisc[:, 12, :].rearrange("p (k c) -> p k c", c=8)
                for kc in range(16):
                    P.add("pe", o_mm(ps[0:8, :], wf[:, kc, :], xn[:, kc, n * TW:(n + 1) * TW], kc == 0, kc == 15),
                          reads=[t_wmisc, t_xn[n]], writes=[tp])
                e_, te_ = TMP()
                P.add("act", o_act(e_[0:8, 0:TW], ps[0:8, :], AF.Exp, bias=small[0:8, 4:5], scale=-1.0), reads=[tp, t_small], writes=[te_])
                P.add("act", o_act(e_[0:8, 0:TW], e_[0:8, 0:TW], AF.Ln, bias=cst[0:8, 3:4]), reads=[te_, t_cst], writes=[te_])
                S, tS = TMP()
                if n == 0:
                    init, rd = 0.0, []
                else:
                    init, rd = prev_S[0][0:8, TW - 1:TW], [prev_S[1]]
                P.add("dve", o_scan(S[0:8, 0:TW], e_[0:8, 0:TW], e_[0:8, 0:TW], init, ALU.add, ALU.max), reads=[te_] + rd, writes=[tS])
                prev_S = (S, tS)
                P.add("dve", o_ts(qa8[:, n * TW:(n + 1) * TW], S[0:8, 0:TW], -8.0, ALU.mult), reads=[tS], writes=[t_qa8])
                k8, tk8 = TMP()
                P.add("dve", o_ts(k8[0:8, 0:TW], S[0:8, 0:TW], 8.0, ALU.mult), reads=[tS], writes=[tk8])
                for b4 in range(4):
                    blk = n * 4 + b4
                    P.add("pe", o_tr(psk[:, blk * 8:(blk + 1) * 8], k8[0:8, b4 * 128:(b4 + 1) * 128], ident[0:8, 0:8]),
                          reads=[tk8, t_cm], writes=[tpk])
                P.add("sp", o_dma(dap(exb_s[li], n * TW, [[1024, 8], [1, TW]]), S[0:8, 0:TW]), reads=[tS], writes=[t_exb_s[li]], dma="exstb")
            P.add("act", o_acopy(kcol[:, 8:16, :], psk[:, 0:64].rearrange("p (b h) -> p b h", b=8)), reads=[tpk], writes=[t_kcol])

            P.add("dve", o_vcopy(tls[:, :, 0:3], xa_pad[:, :, 1025:1028]), reads=t_xa, writes=[t_tls])
            P.add("dve", o_vcopy(tls[:, :, 3:18], xp_pad[:, :, 1025:1040]), reads=t_xp, writes=[t_tls])
            P.add("dve", o_vcopy(tls[:, :, 18:20], u_pad[:, :, 1026:1028]), reads=t_u, writes=[t_tls])
            P.add("sp", o_dma(dap(exs, 512 * 1024, [[80, 128], [1, 80]]), tls[:].rearrange("p c i -> p (c i)")), reads=[t_tls], writes=[t_exa_s[li]], dma="exsta")
            P.add("sp", o_dma(dap(exb_s[li], 8 * 1024, [[4, 128], [1, 4]]), small[:, 12:16]), reads=[t_small], writes=[t_exb_s[li]], dma="exstb")

            if stop == 'proj':
                raise _Stop()
            P.add("pool", (lambda exs=exs, exd=exd: lambda e: e.collective_compute("AllGather", ALU.bypass, replica_groups=groups, ins=[exs.ap()], outs=[exd.ap()]))(),
                  reads=[t_exa_s[li]], writes=[t_exa_d[li]], dma=f"cca{li}", inc=1)
            P.add("pool", (lambda a=exv_s[li], b=exv_d[li]: lambda e: e.collective_compute("AllGather", ALU.bypass, replica_groups=groups, ins=[a.ap()], outs=[b.ap()]))(),
                  reads=[t_exv_s[li]], writes=[t_exv_d[li]], dma=f"ccv{li}", inc=1)
            P.add("pool", (lambda a=exb_s[li], b=exb_d[li]: lambda e: e.collective_compute("AllGather", ALU.bypass, replica_groups=groups, ins=[a.ap()], outs=[b.ap()]))(),
                  reads=[t_exb_s[li]], writes=[t_exb_d[li]], dma=f"ccb{li}", inc=1)

            if stop == 'exch':
                raise _Stop()
            P.add("sp", o_dma(tlr[:].rearrange("p c i -> p (c i)"), dap(exd, 512 * 1024, [[80, 128], [1, 80]])), reads=[t_exa_d[li]], writes=[t_tlr], dma="exld_t")
            P.add("sp", o_dma(small[:, 20:24], dap(exb_d[li], 8 * 1024, [[4, 128], [1, 4]])), reads=[t_exb_d[li]], writes=[t_small], dma="exld_h")
            P.add("dve", o_ts(small[:, 8:12], small[:, 20:24], flag, ALU.mult), reads=[t_small, t_cst], writes=[t_small])
            P.add("dve", o_ts(xa_pad[:, :, 1:4], tlr[:, :, 0:3], flag, ALU.mult), reads=[t_tlr, t_cst], writes=t_xa)
            P.add("dve", o_ts(xp_pad[:, :, 1:16], tlr[:, :, 3:18], flag, ALU.mult), reads=[t_tlr, t_cst], writes=t_xp)
            P.add("dve", o_ts(u_pad[:, :, 2:4], tlr[:, :, 18:20], flag, ALU.mult), reads=[t_tlr, t_cst], writes=t_u)
            Sp = []
            for n in range(NT):
                s_, ts_ = TMP()
                P.add("sp", o_dma(s_[0:8, 0:TW], dap(exb_d[li], n * TW, [[1024, 8], [1, TW]])), reads=[t_exb_d[li]], writes=[ts_], dma=f"exld_s{n}")
                Sp.append((s_, ts_))
            psk2, tpk2 = PS()
            for n in range(NT):
                s_, ts_ = Sp[n]
                k8, tk8 = TMP()
                P.add("dve", o_ts(k8[0:8, 0:TW], s_[0:8, 0:TW], Sp[1][0][0:8, TW - 1:TW], ALU.subtract, 8.0, ALU.mult),
                      reads=[ts_, Sp[1][1]], writes=[tk8])
                P.add("dve", o_ts(k8[0:8, 0:TW], k8[0:8, 0:TW], negbig8[0:8, :], ALU.add), reads=[tk8, t_cst], writes=[tk8])
                for b4 in range(4):
                    blk = n * 4 + b4
                    P.add("pe", o_tr(psk2[:, blk * 8:(blk + 1) * 8], k8[0:8, b4 * 128:(b4 + 1) * 128], ident[0:8, 0:8]),
                          reads=[tk8, t_cm], writes=[tpk2])
            P.add("act", o_acopy(kcol[:, 0:8, :], psk2[:, 0:64].rearrange("p (b h) -> p b h", b=8)), reads=[tpk2], writes=[t_kcol])

            if stop == 'recv':
                raise _Stop()
            for c in range(4):
                lru_chunk(c, True)
            dbg("ya", ys[0], t_y[0])

            for g in range(4):
                win = 2 << g
                for n in range(NT):
                    lo = 16 + n * TW - 15
                    L = TW + 15
                    cur = (xp_pad[:, g, lo:lo + L], t_xp[g], 0)
                    sh = 1
                    for step in range(g + 1):
                        s_, ts_ = TMP()
                        ap, tk_, vf = cur
                        nv = vf + sh
                        P.add("dve", o_tt(s_[:, nv:L], ap[:, nv:L] if step == 0 else ap[:, nv:L], (ap[:, nv - sh:L - sh]), ALU.add),
                              reads=[tk_], writes=[ts_])
                        cur = (s_[:, 0:L], ts_, nv)
                        sh *= 2
                    s_ap, ts_, vf = cur
                    assert vf <= 15
                    if n == 0:
                        P.add("dve", o_tt(s_ap[:, 15:31], s_ap[:, 15:31], cst[:, 4 + g * 16:4 + g * 16 + 16], ALU.mult), reads=[ts_, t_cst], writes=[ts_])
                    pb, tpb = TMP()
                    pbb = pb[:, 0:256].bitcast(BF16)
                    P.add("dve", o_stt(pbb, s_ap[:, 15:15 + TW], 1.0 / win, xp_pad[:, g, 16 + n * TW:16 + (n + 1) * TW], ALU.mult, ALU.subtract),
                          reads=[ts_, t_xp[g]], writes=[tpb])
                    ps, tp = PS()
                    P.add("pe", o_mm(ps[:, :], wmisc[:, 8 + g, :], pbb, True, True), reads=[tpb, t_wmisc], writes=[tp])
                    P.add("act", o_act(ys[1][:, g, n * TW:(n + 1) * TW], ps[:, :], AF.Identity, scale=pc(64 + g)), reads=[tp, t_prm], writes=[t_y[1][g]])
            dbg("yb", ys[1], t_y[1])

            for c in range(4):
                for n in range(NT):
                    v_, tv_ = TMP()
                    base = 2 + n * TW
                    P.add("dve", o_ts(v_[:, 0:TW], u_pad[:, c, base:base + TW], pc(68 + c), ALU.mult), reads=[t_u[c], t_prm], writes=[tv_])
                    for k in range(1, 3):
                        P.add("dve", o_stt(v_[:, 0:TW], u_pad[:, c, base + k:base + k + TW], pc(68 + k * 4 + c), v_[:, 0:TW], ALU.mult, ALU.add),
                              reads=[t_u[c], t_prm, tv_], writes=[tv_])
                    P.add("dve", o_tt(ys[2][:, c, n * TW:(n + 1) * TW], v_[:, 0:TW], gb_raw[:, c, n * TW:(n + 1) * TW], ALU.mult),
                          reads=[tv_, t_gb[c]], writes=[t_y[2][c]])
            dbg("yc", ys[2], t_y[2])

            if stop == 'mixabc':
                raise _Stop()
            for j in range(4):
                bi_ = j % 2
                kT, tkT = kT_b[bi_], t_kT[bi_]
                vv, tvv = v_b[bi_], t_v[bi_]
                qq, tqq = q_b[bi_], t_q[bi_]
                P.add("sp", o_dma(kT[:, 0:1024], dap(exd, j * 128 * 1024, [[1024, 128], [1, 1024]])), reads=[t_exa_d[li]], writes=[tkT], dma=f"k{bi_}")
                P.add("sp", o_dma(kT[:, 1024:2048], dap(exs, j * 128 * 1024, [[1024, 128], [1, 1024]])), reads=[t_exa_s[li]], writes=[tkT], dma=f"k{bi_}")
                voff = j * 8 * 128 * 128
                P.add("sp", o_dma(vv[:, 0:8, :], dap(exv_d[li], voff, [[128, 128], [128 * 128, 8], [1, 128]])), reads=[t_exv_d[li]], writes=[tvv], dma=f"v{bi_}")
                P.add("sp", o_dma(vv[:, 8:16, :], dap(exv_s[li], voff, [[128, 128], [128 * 128, 8], [1, 128]])), reads=[t_exv_s[li]], writes=[tvv], dma=f"v{bi_}")

                def ev_q(n, ps, tp, qq=qq, tqq=tqq):
                    sq, tsq = TMP()
                    sqb = sq[:, 0:256].bitcast(BF16)
                    P.add("act", o_act(sqb, ps[:, :], AF.Square), reads=[tp], writes=[tsq])
                    ps2, tp2 = PS()
                    P.add("pe", o_mm(ps2[:, :], bdones, sqb, True, True), reads=[tsq, t_cmb], writes=[tp2])
                    sd, tsd = TMP()
                    P.add("act", o_act(sd[:, 0:TW], ps2[:, :], AF.Sqrt, bias=eps_c, scale=1.0 / 64), reads=[tp2, t_cst], writes=[tsd])
                    P.add("dve", o_recip(sd[:, 0:TW], sd[:, 0:TW]), reads=[tsd], writes=[tsd])
                    P.add("dve", o_stt(qq[:, n * TW:(n + 1) * TW], ps[:, :], pc(80), sd[:, 0:TW], ALU.mult, ALU.mult),
                          reads=[tp, tsd, t_prm], writes=[tqq])
                proj_fm(layer, ("in", 2560 + j * 128), ev_q)

                for e2 in range(2):
                    h = 2 * j + e2
                    ct, tct = ct_b[e2], t_ct[e2]
                    for n in range(NT):
                        psc, tpc = PSLO()
                        P.add("pe", o_mm(psc[:, :], self_[:, h, :], qa8[:, n * TW:(n + 1) * TW], True, True), reads=[t_sel, t_qa8], writes=[tpc])
                        P.add("act", o_acopy(ct[:, n * TW:(n + 1) * TW], psc[:, :]), reads=[tpc], writes=[tct])
                blist = []
                for e2 in range(2):
                    for n in range(NT):
                        blocks = list(range(8)) + [8 + b_ for b_ in range(4 * n + 4)]
                        for bi2, blk in enumerate(blocks):
                            blist.append((e2, n, bi2, blk, len(blocks)))
                acc_state = {}
                st1 = {}

                def stage1(idx):
                    e2, n, bi2, blk, nb = blist[idx]
                    h = 2 * j + e2
                    R = slice(e2 * 64, e2 * 64 + 64)
                    ct, tct = ct_b[e2], t_ct[e2]
                    own = blk - 8
                    c0 = 0
                    diag = False
                    if own >= 4 * n:
                        c0 = (own - 4 * n) * 128
                        diag = True
                    ncol = TW - c0
                    pss, tpss = PSLO()
                    P.add("pe", o_mm(pss[:, 0:ncol], kT[R, blk * 128:(blk + 1) * 128], qq[R, n * TW + c0:(n + 1) * TW], True, True),
                          reads=[tkT, tqq], writes=[tpss])
                    z, tz = TMP()
                    P.add("dve", o_stt(z[:, 0:ncol], pss[:, 0:ncol], kcol[:, blk, h:h + 1], ct[:, n * TW + c0:(n + 1) * TW], ALU.add, ALU.add),
                          reads=[tpss, t_kcol, tct], writes=[tz])
                    pi = idx % 4
                    pT, tpT = pT_b[pi], t_pT[pi]
                    P.add("act", o_act(pT[:, 0:ncol], z[:, 0:ncol], AF.Exp, scale=0.125), reads=[tz], writes=[tpT])
                    if diag:
                        P.add("dve", o_tt(pT[:, 0:128], pT[:, 0:128], mask_bf, ALU.mult), reads=[tpT, t_cmb], writes=[tpT])
                    st1[idx] = (pT, tpT, c0, ncol)

                def stage2(idx):
                    e2, n, bi2, blk, nb = blist[idx]
                    R = slice(e2 * 64, e2 * 64 + 64)
                    pT, tpT, c0, ncol = st1.pop(idx)
                    if bi2 == 0:
                        acc_state[(e2, n)] = (PSHI(), PSHI())
                    (pnum, tpn), (pden, tpd) = acc_state[(e2, n)]
                    first = bi2 == 0
                    last = bi2 == nb - 1
                    P.add("pe", o_mm(pnum[:, c0:TW], vv[:, blk, :], pT[:, 0:ncol], first, last), reads=[tvv, tpT], writes=[tpn])
                    P.add("pe", o_mm(pden[:, c0:TW], ones_bf, pT[:, 0:ncol], first, last), reads=[t_cmb, tpT], writes=[tpd])
                    if last:
                        rd_, trd = TMP()
                        P.add("dve", o_recip(rd_[R, 0:TW], pden[R, :]), reads=[tpd], writes=[trd])
                        P.add("dve", o_tt(ys[3][R, j, n * TW:(n + 1) * TW], pnum[R, :], rd_[R, 0:TW], ALU.mult), reads=[tpn, trd], writes=[t_y[3][j]])

                LOOK = 2
                nbl = len(blist)
                for idx in range(min(LOOK, nbl)):
                    stage1(idx)
                for idx in range(nbl):
                    stage2(idx)
                    if idx + LOOK < nbl:
                        stage1(idx + LOOK)
            dbg("yd", ys[3], t_y[3])

            if stop == 'attn':
                raise _Stop()
            merged = R2
            for j in range(16):
                wb, twb = load_w(layer, ("branch", j))
                accs = [TMP() for _ in range(NT)]
                for k in range(4):
                    wg, twg = load_w(layer, ("in", 4104 + k * 2048 + j * 128))
                    for n in range(NT):
                        psg, tpg = PS()
                        for kc in range(16):
                            P.add("pe", o_mm(psg[:, :], wg[:, kc, :], xn[:, kc, n * TW:(n + 1) * TW], kc == 0, kc == 15), reads=[twg, t_xn[n]], writes=[tpg])
                        psb_, tpb_ = PS()
                        for kc in range(4):
                            P.add("pe", o_mm(psb_[:, :], wb[:, k * 4 + kc, :], ys[k][:, kc, n * TW:(n + 1) * TW], kc == 0, kc == 3),
                                  reads=[twb, t_y[k][kc]], writes=[tpb_])
                        sg, tsg = TMP()
                        P.add("act", o_act(sg[:, 0:TW], psg[:, :], AF.Sigmoid), reads=[tpg], writes=[tsg])
                        acc, tacc = accs[n]
                        if k == 0:
                            P.add("dve", o_tt(acc[:, 0:TW], psb_[:, :], sg[:, 0:TW], ALU.mult), reads=[tpb_, tsg], writes=[tacc])
                        else:
                            P.add("dve", o_tt(sg[:, 0:TW], psb_[:, :], sg[:, 0:TW], ALU.mult), reads=[tpb_, tsg], writes=[tsg])
                            if k < 3:
                                P.add("dve", o_tt(acc[:, 0:TW], acc[:, 0:TW], sg[:, 0:TW], ALU.add), reads=[tacc, tsg], writes=[tacc])
                            else:
                                P.add("dve", o_tt(merged[:, j, n * TW:(n + 1) * TW], acc[:, 0:TW], sg[:, 0:TW], ALU.add), reads=[tacc, tsg], writes=[t_R2[j][n]])
            dbg("merged", merged[:], [t_R2[c][n] for c in range(16) for n in range(NT)])

            if stop == 'merge':
                raise _Stop()
            P.add("sp", o_dma(xT, xsp), reads=[t_xsp], writes=[t_xT[c][n] for c in range(16) for n in range(NT)], dma="unspill")

            for i in range(16):
                wo, two = load_w(layer, ("out", i))
                for n in range(NT):
                    ps, tp = PS()
                    for jc in range(16):
                        P.add("pe", o_mm(ps[:, :], wo[:, jc, :], merged[:, jc, n * TW:(n + 1) * TW], jc == 0, jc == 15), reads=[two, t_R2[jc][n]], writes=[tp])
                    P.add("dve", o_tt(xT[:, i, n * TW:(n + 1) * TW], xT[:, i, n * TW:(n + 1) * TW], ps[:, :], ALU.add), reads=[tp, t_xT[i][n]], writes=[t_xT[i][n]])

            if stop == 'wout':
                raise _Stop()
            rmsnorm_to_xn(li, 16)
            hq = R2
            for q in range(4):
                for hc in range(16):
                    wu, twu = load_w(layer, ("up", q * 16 + hc))
                    for n in range(NT):
                        ps, tp = PS()
                        for kc in range(16):
                            P.add("pe", o_mm(ps[:, :], wu[:, kc, :], xn[:, kc, n * TW:(n + 1) * TW], kc == 0, kc == 15), reads=[twu, t_xn[n]], writes=[tp])
                        r_, tr_ = TMP()
                        P.add("act", o_act(r_[:, 0:TW], ps[:, :], AF.Relu), reads=[tp], writes=[tr_])
                        P.add("dve", o_tt(hq[:, hc, n * TW:(n + 1) * TW], r_[:, 0:TW], r_[:, 0:TW], ALU.mult), reads=[tr_], writes=[t_R2[hc][n]])
                for i in range(16):
                    wd_, twd = load_w(layer, ("down", q, i))
                    for n in range(NT):
                        ps, tp = PS()
                        for hc in range(16):
                            P.add("pe", o_mm(ps[:, :], wd_[:, hc, :], hq[:, hc, n * TW:(n + 1) * TW], hc == 0, hc == 15), reads=[twd, t_R2[hc][n]], writes=[tp])
                        P.add("dve", o_tt(xT[:, i, n * TW:(n + 1) * TW], xT[:, i, n * TW:(n + 1) * TW], ps[:, :], ALU.add), reads=[tp, t_xT[i][n]], writes=[t_xT[i][n]])


        try:
            for li, layer in enumerate(layers):
                layer_body(li, layer)
        except _Stop:
            pass

        for c4 in range(4):
            P.add("sp", o_dma(yout[:, c4 * 4:(c4 + 1) * 4, :], xT[:, c4 * 4:(c4 + 1) * 4, :]),
                  reads=[t_xT[c][n] for c in range(c4 * 4, c4 * 4 + 4) for n in range(NT)], writes=[t_yout], dma="out")
        fw = ["out"]
        for name in dbg_list:
            fw.append("dbg_" + name)
        P.emit(final_waits=fw)
    return nc


_CACHE = {}


def make_in_maps(inp, x_override=None):
    W = pack_weights(inp)
    prm = pack_params(inp)
    cm, sel = shared_consts()
    x = np.asarray(inp["x"], np.float32) if x_override is None else x_override
    in_maps = []
    for core in range(8):
        b, half = core // 2, core % 2
        xs = x[b, half * T:(half + 1) * T, :]
        xTc = np.ascontiguousarray(xs.T.reshape(16, 128, T).transpose(1, 0, 2))
        in_maps.append({"xT": xTc, "W": W.reshape(DEPTH * NCH, 128, 16, 128), "prm": prm,
                        "cst": core_consts(half), "cmat": cm, "sel": sel})
    return in_maps


def gather_out(results, key="yT"):
    out = np.zeros((4, 2 * T, D), np.float32)
    for core in range(8):
        b, half = core // 2, core % 2
        yT = np.asarray(results[core][key])
        out[b, half * T:(half + 1) * T, :] = yT.transpose(1, 0, 2).reshape(D, T).T
    return out


def kernel(**inputs):
    inp = {k: np.asarray(v) for k, v in inputs.items()}
    if "nc" not in _CACHE:
        _CACHE["nc"] = build(layers=(0, 1))
    nc = _CACHE["nc"]
    in_maps = make_in_maps(inp)
    res = run_bass_kernel_spmd(nc, in_maps, core_ids=list(range(8)))
    return gather_out(res.results)
```

```python
import contextlib
import numpy as np
import ml_dtypes
import concourse.bass as bass
import concourse.mybir as mybir
from concourse.bass_utils import run_bass_kernel_spmd

F32 = mybir.dt.float32
BF16 = mybir.dt.bfloat16
AF = mybir.ActivationFunctionType
ALU = mybir.AluOpType

D = 2048
T = 1024
TW = 512
NT = 2
DEPTH = 2
NPRM = 96
NCST = 72
EPS = 1e-6


class Tok:
    __slots__ = ("name", "grp", "lo", "hi", "writer", "readers")

    def __init__(self, name, grp=None, lo=0, hi=0):
        self.name = name
        self.grp = grp
        self.lo = lo
        self.hi = hi
        self.writer = None
        self.readers = {}


class Op:
    __slots__ = ("eng", "idx", "fn", "deps", "signal", "val", "dma_sem", "dma_val", "waits", "inc", "tag")


class Prog:
    ENGS = ("pe", "act", "dve", "pool", "sp")

    def __init__(self, nc):
        self.nc = nc
        self.ops = {e: [] for e in self.ENGS}
        self.groups = {}
        self.dma_count = {}
        self.sync_same = {"act": True, "dve": True, "pool": True, "pe": False, "sp": False}

    def tok(self, name, grp=None, lo=0, hi=0):
        t = Tok(name, grp, lo, hi)
        if grp is not None:
            self.groups.setdefault(grp, []).append(t)
        return t

    def _overl(self, t):
        if t.grp is None:
            return (t,)
        return [u for u in self.groups[t.grp] if u.lo < t.hi and t.lo < u.hi]

    def add(self, eng, fn, reads=(), writes=(), dma=None, inc=16):
        op = Op()
        op.eng = eng
        op.idx = len(self.ops[eng])
        op.fn = fn
        op.signal = False
        op.val = None
        op.dma_sem = dma
        op.dma_val = None
        op.waits = None
        op.inc = inc
        import sys as _sys
        fr = _sys._getframe(1)
        op.tag = (fr.f_lineno, fr.f_back.f_lineno if fr.f_back else 0)
        deps = set()
        for r in reads:
            for u in self._overl(r):
                if u.writer is not None:
                    deps.add(u.writer)
        for w in writes:
            for u in self._overl(w):
                if u.writer is not None:
                    deps.add(u.writer)
                for o in u.readers.values():
                    deps.add(o)
        op.deps = deps
        if dma is not None:
            self.dma_count[dma] = self.dma_count.get(dma, 0) + inc
            op.dma_val = self.dma_count[dma]
        for r in reads:
            key = eng if dma is None else ("dma", dma)
            r.readers[key] = op
        for w in writes:
            w.writer = op
            w.readers = {}
        self.ops[eng].append(op)
        return op

    def finalize(self):
        for e in self.ENGS:
            seen_eng = {}
            seen_dma = {}
            for op in self.ops[e]:
                best_e = {}
                best_d = {}
                for d in op.deps:
                    if d.dma_sem is not None:
                        if seen_dma.get(d.dma_sem, 0) >= d.dma_val:
                            continue
                        if best_d.get(d.dma_sem) is None or best_d[d.dma_sem].dma_val < d.dma_val:
                            best_d[d.dma_sem] = d
                    else:
                        if d.eng == e and (not self.sync_same[e] or d.idx >= op.idx):
                            continue
                        if seen_eng.get(d.eng, -1) >= d.idx:
                            continue
                        if best_e.get(d.eng) is None or best_e[d.eng].idx < d.idx:
                            best_e[d.eng] = d
                for k, d in best_e.items():
                    seen_eng[k] = d.idx
                    d.signal = True
                for k, d in best_d.items():
                    seen_dma[k] = d.dma_val
                op.waits = list(best_e.values()) + list(best_d.values())
        for e in self.ENGS:
            c = 0
            for op in self.ops[e]:
                if op.dma_sem is None and op.signal:
                    c += 1
                    op.val = c

    def emit(self, final_waits=()):
        nc = self.nc
        self.finalize()
        with contextlib.ExitStack() as st:
            esem = {e: st.enter_context(nc.semaphore("s_" + e)) for e in self.ENGS}
            dsem = {n: st.enter_context(nc.semaphore("d_" + n)) for n in self.dma_count}
            block = st.enter_context(nc.Block())
            binder = {"pe": block.tensor, "act": block.scalar, "dve": block.vector,
                      "pool": block.gpsimd, "sp": block.sync}

            def run(e, eng):
                for op in self.ops[e]:
                    for d in op.waits:
                        if d.dma_sem is not None:
                            eng.wait_ge(dsem[d.dma_sem], d.dma_val)
                        else:
                            eng.wait_ge(esem[d.eng], d.val)
                    ins = op.fn(eng)
                    import os as _os
                    if _os.environ.get("DBG_INS") and getattr(getattr(ins, "ins", None), "name", None) == _os.environ["DBG_INS"]:
                        print("DBG_INS", op.eng, op.idx, op.tag)
                    if op.dma_sem is not None:
                        ins.then_inc(dsem[op.dma_sem], op.inc)
                    elif op.signal:
                        ins.then_inc(esem[e], 1)
                if e == "sp":
                    for n in final_waits:
                        eng.wait_ge(dsem[n], self.dma_count[n])

            for e in self.ENGS:
                def mk(e):
                    def body(eng):
                        run(e, eng)
                    return body
                binder[e](mk(e))


def o_mm(out, lhsT, rhs, start, stop):
    return lambda e: e.matmul(out, lhsT, rhs, start=start, stop=stop)


def o_tr(out, in_, ident):
    return lambda e: e.matmul(out, in_, ident, start=True, stop=True, is_transpose=True)


def o_act(out, in_, func, bias=None, scale=None):
    def f(e):
        kw = {}
        if bias is not None:
            kw["bias"] = bias
        if scale is not None:
            kw["scale"] = scale
        return e.activation(out, in_, func, **kw)
    return f


def o_tt(out, a, b, op):
    return lambda e: e.tensor_tensor(out=out, in0=a, in1=b, op=op)


def o_ts(out, a, s1, op0, s2=None, op1=None):
    if op1 is None:
        return lambda e: e.tensor_scalar(out=out, in0=a, scalar1=s1, scalar2=None, op0=op0)
    return lambda e: e.tensor_scalar(out=out, in0=a, scalar1=s1, scalar2=s2, op0=op0, op1=op1)


def o_stt(out, in0, scalar, in1, op0, op1):
    return lambda e: e.scalar_tensor_tensor(out=out, in0=in0, scalar=scalar, in1=in1, op0=op0, op1=op1)


def o_scan(out, d0, d1, init, op0, op1):
    return lambda e: e.tensor_tensor_scan(out=out, data0=d0, data1=d1, initial=init, op0=op0, op1=op1)


def o_vcopy(out, in_):
    return lambda e: e.tensor_copy(out=out, in_=in_)


def o_acopy(out, in_):
    return lambda e: e.copy(out, in_)


def o_recip(out, in_):
    return lambda e: e.reciprocal(out=out, in_=in_)


def o_memset(ap, v):
    return lambda e: e.memset(ap, v)


def o_dma(out, in_):
    return lambda e: e.dma_start(out=out, in_=in_)


def chunk_order():
    specs = [("misc",)]
    for c in range(4):
        specs.append(("in", 0 + c * 128))
    for c in range(4):
        specs.append(("in", 512 + c * 128))
    for c in range(4):
        specs.append(("in", 1024 + 512 + c * 128))
        specs.append(("in", 1024 + 1024 + c * 128))
        specs.append(("in", 1024 + c * 128))
    for c in range(4):
        specs.append(("in", 2560 + 512 + c * 128))
    for c in range(4):
        specs.append(("in", 2560 + 1024 + c * 128))
    for c in range(4):
        specs.append(("in", 2560 + c * 128))
    for j in range(16):
        specs.append(("branch", j))
        for k in range(4):
            specs.append(("in", 4104 + k * 2048 + j * 128))
    for i in range(16):
        specs.append(("out", i))
    for q in range(4):
        for hc in range(16):
            specs.append(("up", q * 16 + hc))
        for i in range(16):
            specs.append(("down", q, i))
    return specs


NCH = len(chunk_order())


def pack_weights(inp):
    specs = chunk_order()
    W = np.zeros((DEPTH * NCH, 128, 16, 128), np.float32)

    def std(mat):
        return mat.reshape(16, 128, 128).transpose(1, 0, 2)

    for l in range(DEPTH):
        w_in = inp["w_in"][l]
        for n, s in enumerate(specs):
            dst = W[l * NCH + n]
            if s[0] == "misc":
                for ri, name in enumerate(("lru_wr", "lru_wi")):
                    wl = inp[name][l]
                    for c in range(4):
                        dst[0:64, ri * 4 + c, 0:64] = wl[2 * c]
                        dst[64:128, ri * 4 + c, 64:128] = wl[2 * c + 1]
                for g in range(4):
                    dst[:, 8 + g, :] = inp["pool_w"][l][g]
                wf = w_in[:, 4096:4104]
                dst[:, 12, :] = wf.reshape(16, 128, 8).transpose(1, 0, 2).reshape(128, 128)
            elif s[0] == "in":
                dst[:] = std(w_in[:, s[1]:s[1] + 128])
            elif s[0] == "branch":
                j = s[1]
                wb = inp["w_branch"][l][:, :, j * 128:(j + 1) * 128]
                dst[:] = wb.reshape(4, 4, 128, 128).transpose(2, 0, 1, 3).reshape(128, 16, 128)
            elif s[0] == "out":
                i = s[1]
                dst[:] = std(inp["w_out"][l][:, i * 128:(i + 1) * 128])
            elif s[0] == "up":
                j = s[1]
                dst[:] = std(inp["w_mlp_up"][l][:, j * 128:(j + 1) * 128])
            elif s[0] == "down":
                q, i = s[1], s[2]
                dst[:] = std(inp["w_mlp_down"][l][q * 2048:(q + 1) * 2048, i * 128:(i + 1) * 128])
    return W.reshape(DEPTH * NCH, 128, 2048)


def pack_params(inp):
    prm = np.zeros((DEPTH, 128, NPRM), np.float32)

    def col(v):
        return v.reshape(-1, 128).T

    for l in range(DEPTH):
        p = prm[l]
        p[:, 0:16] = col(inp["norm_mix_g"][l])
        p[:, 16:32] = col(inp["norm_mlp_g"][l])
        for k in range(4):
            p[:, 32 + k * 4:36 + k * 4] = col(inp["lru_conv_w"][l][k])
        p[:, 48:52] = col(inp["lru_conv_b"][l])
        p[:, 52:56] = col(inp["lru_br"][l])
        p[:, 56:60] = col(inp["lru_bi"][l])
        p[:, 60:64] = col(inp["lru_lambda"][l])
        p[:, 64:68] = col(inp["pool_scale"][l])
        for k in range(3):
            p[:, 68 + k * 4:72 + k * 4] = col(inp["sconv_w"][l][k])
        p[:, 80] = np.tile(inp["q_norm_g"][l], 2)
        p[:, 81] = np.tile(inp["k_norm_g"][l], 2)
        p[0:8, 82] = inp["forget_b"][l]
    return prm


def core_consts(half):
    cst = np.zeros((128, NCST), np.float32)
    cst[:, 0] = float(half)
    cst[:, 1] = 0.0 if half else -8.0e4
    cst[:, 2] = EPS
    cst[:, 3] = 1.0
    for g, win in enumerate((2, 4, 8, 16)):
        for t in range(16):
            cst[:, 4 + g * 16 + t] = 1.0 if half else float(win) / float(min(t + 1, win))
    return cst


def shared_consts():
    cm = np.zeros((128, 3, 128), np.float32)
    s = np.arange(128)[:, None]
    t = np.arange(128)[None, :]
    cm[:, 0, :] = (s <= t).astype(np.float32)
    cm[0:64, 1, 0:64] = 1.0
    cm[64:128, 1, 64:128] = 1.0
    cm[:, 2, :] = np.eye(128, dtype=np.float32)
    sel = np.zeros((8, 8, 128), np.float32)
    for h in range(8):
        sel[h, h, :] = 1.0
    return cm, sel


class _Stop(Exception):
    pass


def build(layers=(0, 1), debug=(), stop=None, ncores=8):
    nc = bass.Bass("TRN2", target_bir_lowering=False)
    nL = len(layers)
    xin = nc.dram_tensor("xT", [128, 16, T], F32, kind="ExternalInput").ap()
    Wd = nc.dram_tensor("W", [nL * NCH, 128, 16, 128], F32, kind="ExternalInput").ap()
    prmd = nc.dram_tensor("prm", [DEPTH, 128, NPRM], F32, kind="ExternalInput").ap()
    cstd = nc.dram_tensor("cst", [128, NCST], F32, kind="ExternalInput").ap()
    cmd = nc.dram_tensor("cmat", [128, 3, 128], F32, kind="ExternalInput").ap()
    seld = nc.dram_tensor("sel", [8, 8, 128], F32, kind="ExternalInput").ap()
    yout = nc.dram_tensor("yT", [128, 16, T], F32, kind="ExternalOutput").ap()
    xsp = nc.dram_tensor("xspill", [128, 16, T], F32, kind="Internal").ap()
    EXA = 522
    exa_s = [nc.dram_tensor(f"exa_s{l}", [EXA, 1024], BF16, kind="Internal") for l in range(nL)]
    exa_d = [nc.dram_tensor(f"exa_d{l}", [2 * EXA, 1024], BF16, kind="Internal") for l in range(nL)]
    exv_s = [nc.dram_tensor(f"exv_s{l}", [512, 1024], BF16, kind="Internal") for l in range(nL)]
    exv_d = [nc.dram_tensor(f"exv_d{l}", [1024, 1024], BF16, kind="Internal") for l in range(nL)]
    exb_s = [nc.dram_tensor(f"exb_s{l}", [17, 512], F32, kind="Internal") for l in range(nL)]
    exb_d = [nc.dram_tensor(f"exb_d{l}", [34, 512], F32, kind="Internal") for l in range(nL)]
    dbg_out = {}
    for name, shape, dt in debug:
        dbg_out[name] = nc.dram_tensor("dbg_" + name, list(shape), dt, kind="ExternalOutput").ap()
    groups = [[2 * g, 2 * g + 1] for g in range(ncores // 2)]

    def dap(th, off, dims):
        return bass.AP(th, off, [list(d) for d in dims])

    st = contextlib.ExitStack()
    with st:
        P = Prog(nc)

        def sb(name, shape, dt):
            return st.enter_context(nc.sbuf_tensor(name, list(shape), dt))

        R0 = sb("R0", [128, 16 * T + 64], F32)
        xn = sb("xn", [128, 16, T], BF16)
        R2 = sb("R2", [128, 16, T], BF16)
        NSLOT = 4
        wsl = [sb(f"w{i}", [128, 16, 128], BF16) for i in range(NSLOT)]
        wbr = [sb(f"wbr{i}", [128, 16, 128], BF16) for i in range(2)]
        NTMP = 16
        tmpf = [sb(f"tmp{i}", [128, 528], F32) for i in range(NTMP)]
        prm = sb("prm_sb", [128, nL, NPRM], F32)
        cst = sb("cst_sb", [128, NCST], F32)
        cmf = sb("cmf", [128, 3, 128], F32)
        cmb = sb("cmb", [128, 3, 128], BF16)
        self_ = sb("sel_sb", [8, 8, 128], F32)
        qa8 = sb("qa8", [8, T], F32)
        kcol = sb("kcol", [128, 16, 8], F32)
        small = sb("small", [128, 64], F32)
        tls = sb("tls", [128, 4, 20], BF16)
        tlr = sb("tlr", [128, 4, 20], BF16)

        R0b = R0[:].bitcast(BF16)
        xT = R0[:, 0:16 * T].rearrange("p (c t) -> p c t", c=16)
        ys = [R0b[:, k * 4096:(k + 1) * 4096].rearrange("p (c t) -> p c t", c=4) for k in range(4)]
        MB = 16384
        xa_pad = R0b[:, MB:MB + 4 * 1028].rearrange("p (c t) -> p c t", c=4)
        xp_pad = R0b[:, MB + 4112:MB + 4112 + 4 * 1040].rearrange("p (c t) -> p c t", c=4)
        u_pad = R0b[:, MB + 8272:MB + 8272 + 4 * 1028].rearrange("p (c t) -> p c t", c=4)
        gb_raw = R0b[:, MB + 12384:MB + 12384 + 4096].rearrange("p (c t) -> p c t", c=4)
        AB = MB
        R2b = R2[:].rearrange("p c t -> p (c t)")
        kT_b = [R2b[:, 2048 + i * 2048:2048 + (i + 1) * 2048] for i in range(2)]
        v_b = [R2b[:, 6144 + i * 2048:6144 + (i + 1) * 2048].rearrange("p (b c) -> p b c", b=16) for i in range(2)]
        q_b = [R2b[:, 10240 + i * 1024:10240 + (i + 1) * 1024] for i in range(2)]
        ct_b = [R2b[:, 12288 + i * 2048:12288 + (i + 1) * 2048].bitcast(F32) for i in range(2)]
        NPT = 8
        pT_b = [sb(f"pT{i}", [128, 512], BF16)[:] for i in range(NPT)]

        G = "R0"

        def tk(name, lo, hi):
            return P.tok(name, G, lo, hi)

        t_xT = [[tk(f"xT{c}_{n}", (c * T + n * TW) * 4, (c * T + n * TW + TW) * 4) for n in range(NT)] for c in range(16)]
        t_y = [[tk(f"y{k}_{c}", (k * 4096 + c * 1024) * 2, (k * 4096 + c * 1024 + 1024) * 2) for c in range(4)] for k in range(4)]
        b0 = MB * 2
        t_xa = [tk(f"xa{c}", b0 + c * 1028 * 2, b0 + (c + 1) * 1028 * 2) for c in range(4)]
        t_xp = [tk(f"xp{c}", b0 + (4112 + c * 1040) * 2, b0 + (4112 + (c + 1) * 1040) * 2) for c in range(4)]
        t_u = [tk(f"u{c}", b0 + (8272 + c * 1028) * 2, b0 + (8272 + (c + 1) * 1028) * 2) for c in range(4)]
        t_gb = [tk(f"gb{c}", b0 + (12384 + c * 1024) * 2, b0 + (12384 + (c + 1) * 1024) * 2) for c in range(4)]
        t_kT = [P.tok(f"kT{i}", "R2", 4096 + i * 4096, 4096 + (i + 1) * 4096) for i in range(2)]
        t_v = [P.tok(f"v{i}", "R2", 12288 + i * 4096, 12288 + (i + 1) * 4096) for i in range(2)]
        t_q = [P.tok(f"q{i}", "R2", 20480 + i * 2048, 20480 + (i + 1) * 2048) for i in range(2)]
        t_ct = [P.tok(f"ct{i}", "R2", 24576 + i * 4096, 24576 + (i + 1) * 4096) for i in range(2)]
        t_pT = [P.tok(f"pT{i}") for i in range(NPT)]
        t_xn = [P.tok(f"xn{n}") for n in range(NT)]
        t_R2 = [[P.tok(f"R2_{c}_{n}", "R2", (c * 1024 + n * 512) * 2, (c * 1024 + n * 512 + 512) * 2) for n in range(NT)] for c in range(16)]
        t_wmisc_tok = P.tok("wmisc", "R2", 0, 4096)
        t_w = [P.tok(f"w{i}") for i in range(NSLOT)]
        t_wbr = [P.tok(f"wbr{i}") for i in range(2)]
        t_tmp = [P.tok(f"tmp{i}") for i in range(NTMP)]
        t_prm, t_cst, t_cm, t_cmb, t_sel = P.tok("prm"), P.tok("cst"), P.tok("cm"), P.tok("cmb"), P.tok("sel")
        t_qa8, t_kcol, t_small, t_tls, t_tlr = P.tok("qa8"), P.tok("kcol"), P.tok("small"), P.tok("tls"), P.tok("tlr")
        t_xsp = P.tok("xsp")
        t_yout = P.tok("yout")
        t_exa_s = [P.tok(f"exa_s{l}") for l in range(nL)]
        t_exa_d = [P.tok(f"exa_d{l}") for l in range(nL)]
        t_exb_s = [P.tok(f"exb_s{l}") for l in range(nL)]
        t_exv_s = [P.tok(f"exv_s{l}") for l in range(nL)]
        t_exv_d = [P.tok(f"exv_d{l}") for l in range(nL)]
        t_exb_d = [P.tok(f"exb_d{l}") for l in range(nL)]

        psb = [st.enter_context(nc.psum_tensor(f"ps{i}", [128, 512], F32)) for i in range(8)]
        t_ps = [P.tok(f"ps{i}") for i in range(8)]
        ps_ctr = [0]

        def PS():
            i = ps_ctr[0] % 8
            ps_ctr[0] += 1
            return psb[i], t_ps[i]

        lo_ctr = [0]

        def PSLO():
            i = lo_ctr[0] % 4
            lo_ctr[0] += 1
            return psb[i], t_ps[i]

        side_ctr = [0]

        def PS_SIDE():
            return PSLO()

        hi_ctr = [0]

        def PSHI():
            i = 4 + (hi_ctr[0] % 4)
            hi_ctr[0] += 1
            return psb[i], t_ps[i]

        tmp_ctr = [0]

        def TMP():
            i = tmp_ctr[0] % NTMP
            tmp_ctr[0] += 1
            return tmpf[i], t_tmp[i]

        w_ctr = [0]
        specs = chunk_order()

        slot_ctr = [0]
        br_ctr = [0]

        def load_w(layer, expect):
            n = w_ctr[0]
            li = n // NCH
            assert layers[li] == layer and specs[n % NCH][0] == expect[0] and tuple(specs[n % NCH][1:]) == tuple(expect[1:]), (n, specs[n % NCH], expect)
            w_ctr[0] += 1
            if expect[0] == "branch":
                s = br_ctr[0] % 2
                br_ctr[0] += 1
                P.add("pool", o_dma(wbr[s][:], Wd[li * NCH + (n % NCH)]), writes=[t_wbr[s]], dma=f"wbr{s}")
                return wbr[s], t_wbr[s]
            s = slot_ctr[0] % NSLOT
            slot_ctr[0] += 1
            P.add("pool", o_dma(wsl[s][:], Wd[li * NCH + (n % NCH)]), writes=[t_w[s]], dma=f"w{s}")
            return wsl[s], t_w[s]

        P.add("sp", o_dma(prm[:], prmd[layers[0]:layers[0] + nL].rearrange("l p n -> p l n")), writes=[t_prm], dma="c0")
        P.add("sp", o_dma(cst[:], cstd), writes=[t_cst], dma="c1")
        P.add("sp", o_dma(cmf[:], cmd), writes=[t_cm], dma="c2")
        P.add("sp", o_dma(self_[:], seld), writes=[t_sel], dma="c3")
        P.add("dve", o_vcopy(cmb[:, 0:2, :], cmf[:, 0:2, :]), reads=[t_cm], writes=[t_cmb])
        P.add("dve", o_memset(cmb[:, 2, :], 1.0), writes=[t_cmb])
        mask_bf = cmb[:, 0, :]
        bdones = cmb[:, 1, :]
        ones_bf = cmb[:, 2, :]
        ident = cmf[:, 2, :]
        flag = cst[:, 0:1]
        negbig8 = cst[:, 1:2]
        eps_c = cst[:, 2:3]
        one_c = cst[:, 3:4]

        for c4 in range(4):
            P.add("sp", o_dma(xT[:, c4 * 4:(c4 + 1) * 4, :], xin[:, c4 * 4:(c4 + 1) * 4, :]),
                  writes=[t_xT[c][n] for c in range(c4 * 4, c4 * 4 + 4) for n in range(NT)], dma=f"x{c4}")

        def rmsnorm_to_xn(li, gcol0):
            for n in range(NT):
                ps, tp = PS()
                for c in range(16):
                    sq, tsq = TMP()
                    sqb = sq[:, 0:256].bitcast(BF16)
                    P.add("act", o_act(sqb, xT[:, c, n * TW:(n + 1) * TW], AF.Square), reads=[t_xT[c][n]], writes=[tsq])
                    P.add("pe", o_mm(ps[:, :], ones_bf, sqb, c == 0, c == 15), reads=[tsq, t_cmb], writes=[tp])
                sd, tsd = TMP()
                P.add("act", o_act(sd[:, 0:TW], ps[:, :], AF.Sqrt, bias=eps_c, scale=1.0 / D), reads=[tp, t_cst], writes=[tsd])
                rs, trs = TMP()
                P.add("dve", o_recip(rs[:, 0:TW], sd[:, 0:TW]), reads=[tsd], writes=[trs])
                for c in range(16):
                    P.add("dve", o_stt(xn[:, c, n * TW:(n + 1) * TW], xT[:, c, n * TW:(n + 1) * TW],
                                       prm[:, li, gcol0 + c:gcol0 + c + 1], rs[:, 0:TW], ALU.mult, ALU.mult),
                          reads=[t_xT[c][n], trs, t_prm], writes=[t_xn[n]])

        def proj_fm(layer, expect, evac):
            w, tw = load_w(layer, expect)
            for n in range(NT):
                ps, tp = PS()
                for kc in range(16):
                    P.add("pe", o_mm(ps[:, :], w[:, kc, :], xn[:, kc, n * TW:(n + 1) * TW], kc == 0, kc == 15),
                          reads=[tw, t_xn[n]], writes=[tp])
                evac(n, ps, tp)

        dbg_list = []

        def dbg(name, ap, toks):
            if name in dbg_out:
                dbg_list.append(name)
                P.add("sp", o_dma(dbg_out[name], ap), reads=list(toks), writes=[P.tok("dbg_" + name)], dma="dbg_" + name)

        def layer_body(li, layer):
            pl = prm[:, li, :]

            def pc(i):
                return prm[:, li, i:i + 1]

            rmsnorm_to_xn(li, 0)
            P.add("sp", o_dma(xsp, xT), reads=[t_xT[c][n] for c in range(16) for n in range(NT)], writes=[t_xsp], dma="spill")

            if stop == 'norm':
                raise _Stop()
            wm, twm = load_w(layer, ("misc",))
            wmisc = R2[:, 0:2, :].rearrange("p a t -> p (a t)").rearrange("p (s c) -> p s c", s=16)
            t_wmisc = t_wmisc_tok
            t_wm_all = [t_wmisc_tok]
            P.add("dve", o_vcopy(wmisc, wm[:]), reads=[twm], writes=t_wm_all)
            P.add("act", o_act(small[:, 0:4], pl[:, 60:64], AF.Exp, scale=-1.0), reads=[t_prm], writes=[t_small])
            P.add("act", o_act(small[:, 0:4], small[:, 0:4], AF.Ln, bias=one_c), reads=[t_small, t_cst], writes=[t_small])
            P.add("dve", o_ts(small[:, 0:4], small[:, 0:4], -8.0, ALU.mult), reads=[t_small], writes=[t_small])
            P.add("dve", o_ts(small[:, 4:5], pl[:, 82:83], -1.0, ALU.mult), reads=[t_prm, t_small], writes=[t_small])
            P.add("dve", o_memset(xa_pad[:, :, 0:4], 0.0), writes=t_xa)
            P.add("dve", o_memset(xp_pad[:, :, 0:16], 0.0), writes=t_xp)
            P.add("dve", o_memset(u_pad[:, :, 0:4], 0.0), writes=t_u)

            def lru_parts(c, n, final, state):
                loc = {}

                def p1():
                    u, tu = TMP()
                    base = 1 + n * TW
                    P.add("dve", o_ts(u[:, 0:TW], xa_pad[:, c, base:base + TW], pc(32 + c), ALU.mult, pc(48 + c), ALU.add),
                          reads=[t_xa[c], t_prm], writes=[tu])
                    for k in range(1, 4):
                        P.add("dve", o_stt(u[:, 0:TW], xa_pad[:, c, base + k:base + k + TW], pc(32 + k * 4 + c), u[:, 0:TW], ALU.mult, ALU.add),
                              reads=[t_xa[c], t_prm, tu], writes=[tu])
                    ub, tub = TMP()
                    ubb = ub[:, 0:256].bitcast(BF16)
                    P.add("act", o_acopy(ubb, u[:, 0:TW]), reads=[tu], writes=[tub])
                    loc.update(u=u, tu=tu, ubb=ubb, tub=tub)

                def p2():
                    u, tu, ubb, tub = loc["u"], loc["tu"], loc["ubb"], loc["tub"]
                    psr, tpr = PSLO() if final else PS()
                    P.add("pe", o_mm(psr[:, :], wmisc[:, c, :], ubb, True, True), reads=[tub, t_wmisc], writes=[tpr])
                    psi, tpi = PSLO() if final else PS()
                    P.add("pe", o_mm(psi[:, :], wmisc[:, 4 + c, :], ubb, True, True), reads=[tub, t_wmisc], writes=[tpi])
                    r, tr_ = TMP()
                    P.add("act", o_act(r[:, 0:TW], psr[:, :], AF.Sigmoid, bias=pc(52 + c)), reads=[tpr, t_prm], writes=[tr_])
                    gi, tgi = TMP()
                    P.add("act", o_act(gi[:, 0:TW], psi[:, :], AF.Sigmoid, bias=pc(56 + c)), reads=[tpi, t_prm], writes=[tgi])
                    P.add("act", o_act(r[:, 0:TW], r[:, 0:TW], AF.Exp, scale=small[:, c:c + 1]), reads=[tr_, t_small], writes=[tr_])
                    sq, tsq = TMP()
                    P.add("dve", o_tt(sq[:, 0:TW], r[:, 0:TW], r[:, 0:TW], ALU.mult), reads=[tr_], writes=[tsq])
                    P.add("act", o_act(sq[:, 0:TW], sq[:, 0:TW], AF.Sqrt, bias=one_c, scale=-1.0), reads=[tsq, t_cst], writes=[tsq])
                    P.add("dve", o_tt(gi[:, 0:TW], gi[:, 0:TW], u[:, 0:TW], ALU.mult), reads=[tgi, tu], writes=[tgi])
                    P.add("dve", o_tt(gi[:, 0:TW], gi[:, 0:TW], sq[:, 0:TW], ALU.mult), reads=[tgi, tsq], writes=[tgi])
                    h, th = TMP()
                    if n == 0:
                        init = small[:, 8 + c:9 + c] if final else 0.0
                        rd = [t_small] if final else []
                    else:
                        init = state["prev_h"][0][:, TW - 1:TW]
                        rd = [state["prev_h"][1]]
                    P.add("dve", o_scan(h[:, 0:TW], r[:, 0:TW], gi[:, 0:TW], init, ALU.mult, ALU.add),
                          reads=[tr_, tgi] + rd, writes=[th])
                    state["prev_h"] = (h, th)
                    if final:
                        P.add("act", o_acopy(ys[0][:, c, n * TW:(n + 1) * TW], h[:, 0:TW]), reads=[th], writes=[t_y[0][c]])
                    elif n == NT - 1:
                        P.add("act", o_acopy(small[:, 12 + c:13 + c], h[:, TW - 1:TW]), reads=[th], writes=[t_small])
                return p1, p2

            def lru_chunk(c, final):
                stt_ = {}
                for n in range(NT):
                    p1, p2 = lru_parts(c, n, final, stt_)
                    p1()
                    p2()

            pend = []
            wait2 = [None]

            def tick():
                if wait2[0] is not None:
                    wait2[0]()
                    wait2[0] = None
                if pend:
                    p1, p2 = pend.pop(0)
                    p1()
                    wait2[0] = p2

            for c in range(4):
                def ev(n, ps, tp, c=c):
                    P.add("act", o_acopy(xa_pad[:, c, 4 + n * TW:4 + (n + 1) * TW], ps[:, :]), reads=[tp], writes=[t_xa[c]])
                proj_fm(layer, ("in", c * 128), ev)
                stt_c = {}
                for n in range(NT):
                    pend.append(lru_parts(c, n, False, stt_c))
            for c in range(4):
                def ev(n, ps, tp, c=c):
                    P.add("act", o_acopy(xp_pad[:, c, 16 + n * TW:16 + (n + 1) * TW], ps[:, :]), reads=[tp], writes=[t_xp[c]])
                proj_fm(layer, ("in", 512 + c * 128), ev)
                tick()
            for c in range(4):
                gcs = []

                def ev_gc(n, ps, tp):
                    g_, tg_ = TMP()
                    P.add("act", o_acopy(g_[:, 0:TW], ps[:, :]), reads=[tp], writes=[tg_])
                    gcs.append((g_, tg_))
                proj_fm(layer, ("in", 1536 + c * 128), ev_gc)

                def ev_xc(n, ps, tp, c=c):
                    g_, tg_ = gcs[n]
                    P.add("dve", o_tt(u_pad[:, c, 4 + n * TW:4 + (n + 1) * TW], ps[:, :], g_[:, 0:TW], ALU.mult),
                          reads=[tp, tg_], writes=[t_u[c]])
                proj_fm(layer, ("in", 2048 + c * 128), ev_xc)
                tick()

                def ev_gb(n, ps, tp, c=c):
                    P.add("act", o_acopy(gb_raw[:, c, n * TW:(n + 1) * TW], ps[:, :]), reads=[tp], writes=[t_gb[c]])
                proj_fm(layer, ("in", 1024 + c * 128), ev_gb)
                tick()

            if stop == 'projabc':
                raise _Stop()
            while pend or wait2[0] is not None:
                tick()
            exs, exd = exa_s[li], exa_d[li]
            for c in range(4):
                kst, tkst = TMP()
                kstb = kst[:, 0:512].bitcast(BF16)

                def ev_k(n, ps, tp, c=c, kstb=kstb, tkst=tkst):
                    sq, tsq = TMP()
                    sqb = sq[:, 0:256].bitcast(BF16)
                    P.add("act", o_act(sqb, ps[:, :], AF.Square), reads=[tp], writes=[tsq])
                    ps2, tp2 = PS()
                    P.add("pe", o_mm(ps2[:, :], bdones, sqb, True, True), reads=[tsq, t_cmb], writes=[tp2])
                    sd, tsd = TMP()
                    P.add("act", o_act(sd[:, 0:TW], ps2[:, :], AF.Sqrt, bias=eps_c, scale=1.0 / 64), reads=[tp2, t_cst], writes=[tsd])
                    P.add("dve", o_recip(sd[:, 0:TW], sd[:, 0:TW]), reads=[tsd], writes=[tsd])
                    P.add("dve", o_stt(kstb[:, n * TW:(n + 1) * TW], ps[:, :], pc(81), sd[:, 0:TW], ALU.mult, ALU.mult),
                          reads=[tp, tsd, t_prm], writes=[tkst])
                proj_fm(layer, ("in", 2560 + 512 + c * 128), ev_k)
                P.add("sp", o_dma(dap(exs, c * 128 * 1024, [[1024, 128], [1, 1024]]), kstb), reads=[tkst], writes=[t_exa_s[li]], dma="exsta")

            for c in range(4):
                w, tw = load_w(layer, ("in", 2560 + 1024 + c * 128))
                vst, tvst = TMP()
                vstb = vst[:, 0:512].bitcast(BF16).rearrange("p (b c) -> p b c", b=8)
                for half in range(2):
                    ps, tp = PS()
                    for tb in range(4):
                        blk = half * 4 + tb
                        n = blk // 4
                        for kc in range(16):
                            P.add("pe", o_mm(ps[:, tb * 128:(tb + 1) * 128], xn[:, kc, blk * 128:(blk + 1) * 128], w[:, kc, :], kc == 0, kc == 15),
                                  reads=[tw, t_xn[n]], writes=[tp])
                    P.add("act", o_acopy(vstb[:, half * 4:(half + 1) * 4, :], ps[:, :].rearrange("p (b c) -> p b c", b=4)), reads=[tp], writes=[tvst])
                P.add("sp", o_dma(dap(exv_s[li], c * 8 * 128 * 128, [[128, 128], [128 * 128, 8], [1, 128]]), vstb),
                      reads=[tvst], writes=[t_exv_s[li]], dma="exstv")

            prev_S = None
            psk, tpk = PS()
            for n in range(NT):
                ps, tp = PS()
                wf = wmisc[:, 12, :].rearrange("p (k c) -> p k c", c=8)
                for kc in range(16):
                    P.add("pe", o_mm(ps[0:8, :], wf[:, kc, :], xn[:, kc, n * TW:(n + 1) * TW], kc == 0, kc == 15),
                          reads=[t_wmisc, t_xn[n]], writes=[tp])
                e_, te_ = TMP()
                P.add("act", o_act(e_[0:8, 0:TW], ps[0:8, :], AF.Exp, bias=small[0:8, 4:5], scale=-1.0), reads=[tp, t_small], writes=[te_])
                P.add("act", o_act(e_[0:8, 0:TW], e_[0:8, 0:TW], AF.Ln, bias=cst[0:8, 3:4]), reads=[te_, t_cst], writes=[te_])
                S, tS = TMP()
                if n == 0:
                    init, rd = 0.0, []
                else:
                    init, rd = prev_S[0][0:8, TW - 1:TW], [prev_S[1]]
                P.add("dve", o_scan(S[0:8, 0:TW], e_[0:8, 0:TW], e_[0:8, 0:TW], init, ALU.add, ALU.max), reads=[te_] + rd, writes=[tS])
                prev_S = (S, tS)
                P.add("dve", o_ts(qa8[:, n * TW:(n + 1) * TW], S[0:8, 0:TW], -8.0, ALU.mult), reads=[tS], writes=[t_qa8])
                k8, tk8 = TMP()
                P.add("dve", o_ts(k8[0:8, 0:TW], S[0:8, 0:TW], 8.0, ALU.mult), reads=[tS], writes=[tk8])
                for b4 in range(4):
                    blk = n * 4 + b4
                    P.add("pe", o_tr(psk[:, blk * 8:(blk + 1) * 8], k8[0:8, b4 * 128:(b4 + 1) * 128], ident[0:8, 0:8]),
                          reads=[tk8, t_cm], writes=[tpk])
                P.add("sp", o_dma(dap(exb_s[li], n * TW, [[1024, 8], [1, TW]]), S[0:8, 0:TW]), reads=[tS], writes=[t_exb_s[li]], dma="exstb")
            P.add("act", o_acopy(kcol[:, 8:16, :], psk[:, 0:64].rearrange("p (b h) -> p b h", b=8)), reads=[tpk], writes=[t_kcol])

            P.add("dve", o_vcopy(tls[:, :, 0:3], xa_pad[:, :, 1025:1028]), reads=t_xa, writes=[t_tls])
            P.add("dve", o_vcopy(tls[:, :, 3:18], xp_pad[:, :, 1025:1040]), reads=t_xp, writes=[t_tls])
            P.add("dve", o_vcopy(tls[:, :, 18:20], u_pad[:, :, 1026:1028]), reads=t_u, writes=[t_tls])
            P.add("sp", o_dma(dap(exs, 512 * 1024, [[80, 128], [1, 80]]), tls[:].rearrange("p c i -> p (c i)")), reads=[t_tls], writes=[t_exa_s[li]], dma="exsta")
            P.add("sp", o_dma(dap(exb_s[li], 8 * 1024, [[4, 128], [1, 4]]), small[:, 12:16]), reads=[t_small], writes=[t_exb_s[li]], dma="exstb")

            if stop == 'proj':
                raise _Stop()
            P.add("pool", (lambda exs=exs, exd=exd: lambda e: e.collective_compute("AllGather", ALU.bypass, replica_groups=groups, ins=[exs.ap()], outs=[exd.ap()]))(),
                  reads=[t_exa_s[li]], writes=[t_exa_d[li]], dma=f"cca{li}", inc=1)
            P.add("pool", (lambda a=exv_s[li], b=exv_d[li]: lambda e: e.collective_compute("AllGather", ALU.bypass, replica_groups=groups, ins=[a.ap()], outs=[b.ap()]))(),
                  reads=[t_exv_s[li]], writes=[t_exv_d[li]], dma=f"ccv{li}", inc=1)
            P.add("pool", (lambda a=exb_s[li], b=exb_d[li]: lambda e: e.collective_compute("AllGather", ALU.bypass, replica_groups=groups, ins=[a.ap()], outs=[b.ap()]))(),
                  reads=[t_exb_s[li]], writes=[t_exb_d[li]], dma=f"ccb{li}", inc=1)

            if stop == 'exch':
                raise _Stop()
            P.add("sp", o_dma(tlr[:].rearrange("p c i -> p (c i)"), dap(exd, 512 * 1024, [[80, 128], [1, 80]])), reads=[t_exa_d[li]], writes=[t_tlr], dma="exld_t")
            P.add("sp", o_dma(small[:, 20:24], dap(exb_d[li], 8 * 1024, [[4, 128], [1, 4]])), reads=[t_exb_d[li]], writes=[t_small], dma="exld_h")
            P.add("dve", o_ts(small[:, 8:12], small[:, 20:24], flag, ALU.mult), reads=[t_small, t_cst], writes=[t_small])
            P.add("dve", o_ts(xa_pad[:, :, 1:4], tlr[:, :, 0:3], flag, ALU.mult), reads=[t_tlr, t_cst], writes=t_xa)
            P.add("dve", o_ts(xp_pad[:, :, 1:16], tlr[:, :, 3:18], flag, ALU.mult), reads=[t_tlr, t_cst], writes=t_xp)
            P.add("dve", o_ts(u_pad[:, :, 2:4], tlr[:, :, 18:20], flag, ALU.mult), reads=[t_tlr, t_cst], writes=t_u)
            Sp = []
            for n in range(NT):
                s_, ts_ = TMP()
                P.add("sp", o_dma(s_[0:8, 0:TW], dap(exb_d[li], n * TW, [[1024, 8], [1, TW]])), reads=[t_exb_d[li]], writes=[ts_], dma=f"exld_s{n}")
                Sp.append((s_, ts_))
            psk2, tpk2 = PS()
            for n in range(NT):
                s_, ts_ = Sp[n]
                k8, tk8 = TMP()
                P.add("dve", o_ts(k8[0:8, 0:TW], s_[0:8, 0:TW], Sp[1][0][0:8, TW - 1:TW], ALU.subtract, 8.0, ALU.mult),
                      reads=[ts_, Sp[1][1]], writes=[tk8])
                P.add("dve", o_ts(k8[0:8, 0:TW], k8[0:8, 0:TW], negbig8[0:8, :], ALU.add), reads=[tk8, t_cst], writes=[tk8])
                for b4 in range(4):
                    blk = n * 4 + b4
                    P.add("pe", o_tr(psk2[:, blk * 8:(blk + 1) * 8], k8[0:8, b4 * 128:(b4 + 1) * 128], ident[0:8, 0:8]),
                          reads=[tk8, t_cm], writes=[tpk2])
            P.add("act", o_acopy(kcol[:, 0:8, :], psk2[:, 0:64].rearrange("p (b h) -> p b h", b=8)), reads=[tpk2], writes=[t_kcol])

            if stop == 'recv':
                raise _Stop()
            side = []
            for c in range(4):
                stt_f = {}
                for n in range(NT):
                    p1, p2 = lru_parts(c, n, True, stt_f)
                    side.append(p1)
                    side.append(p2)

            def pool_parts(g, n):
                loc = {}
                win = 2 << g

                def q1():
                    lo = 16 + n * TW - 15
                    L = TW + 15
                    cur = (xp_pad[:, g, lo:lo + L], t_xp[g], 0)
                    sh = 1
                    for step in range(g + 1):
                        s_, ts_ = TMP()
                        ap, tk_, vf = cur
                        nv = vf + sh
                        P.add("dve", o_tt(s_[:, nv:L], ap[:, nv:L], ap[:, nv - sh:L - sh], ALU.add), reads=[tk_], writes=[ts_])
                        cur = (s_[:, 0:L], ts_, nv)
                        sh *= 2
                    s_ap, ts_, vf = cur
                    assert vf <= 15
                    if n == 0:
                        P.add("dve", o_tt(s_ap[:, 15:31], s_ap[:, 15:31], cst[:, 4 + g * 16:4 + g * 16 + 16], ALU.mult), reads=[ts_, t_cst], writes=[ts_])
                    pb, tpb = TMP()
                    pbb = pb[:, 0:256].bitcast(BF16)
                    P.add("dve", o_stt(pbb, s_ap[:, 15:15 + TW], 1.0 / win, xp_pad[:, g, 16 + n * TW:16 + (n + 1) * TW], ALU.mult, ALU.subtract),
                          reads=[ts_, t_xp[g]], writes=[tpb])
                    loc.update(pbb=pbb, tpb=tpb)

                def q2():
                    ps, tp = PS_SIDE()
                    P.add("pe", o_mm(ps[:, :], wmisc[:, 8 + g, :], loc["pbb"], True, True), reads=[loc["tpb"], t_wmisc], writes=[tp])
                    P.add("act", o_act(ys[1][:, g, n * TW:(n + 1) * TW], ps[:, :], AF.Identity, scale=pc(64 + g)), reads=[tp, t_prm], writes=[t_y[1][g]])
                return q1, q2

            for g in range(4):
                for n in range(NT):
                    q1, q2 = pool_parts(g, n)
                    side.append(q1)
                    side.append(q2)

            def sconv_unit(c, n):
                def f():
                    v_, tv_ = TMP()
                    base = 2 + n * TW
                    P.add("dve", o_ts(v_[:, 0:TW], u_pad[:, c, base:base + TW], pc(68 + c), ALU.mult), reads=[t_u[c], t_prm], writes=[tv_])
                    for k in range(1, 3):
                        P.add("dve", o_stt(v_[:, 0:TW], u_pad[:, c, base + k:base + k + TW], pc(68 + k * 4 + c), v_[:, 0:TW], ALU.mult, ALU.add),
                              reads=[t_u[c], t_prm, tv_], writes=[tv_])
                    P.add("dve", o_tt(ys[2][:, c, n * TW:(n + 1) * TW], v_[:, 0:TW], gb_raw[:, c, n * TW:(n + 1) * TW], ALU.mult),
                          reads=[tv_, t_gb[c]], writes=[t_y[2][c]])
                return f

            for c in range(4):
                for n in range(NT):
                    side.append(sconv_unit(c, n))

            def side_tick():
                if side:
                    side.pop(0)()

            if stop == 'mixabc':
                while side:
                    side_tick()
                dbg("ya", ys[0], t_y[0])
                dbg("yb", ys[1], t_y[1])
                dbg("yc", ys[2], t_y[2])
            if stop == 'mixabc':
                raise _Stop()
            for j in range(4):
                bi_ = j % 2
                kT, tkT = kT_b[bi_], t_kT[bi_]
                vv, tvv = v_b[bi_], t_v[bi_]
                qq, tqq = q_b[bi_], t_q[bi_]
                P.add("sp", o_dma(kT[:, 0:1024], dap(exd, j * 128 * 1024, [[1024, 128], [1, 1024]])), reads=[t_exa_d[li]], writes=[tkT], dma=f"k{bi_}")
                P.add("sp", o_dma(kT[:, 1024:2048], dap(exs, j * 128 * 1024, [[1024, 128], [1, 1024]])), reads=[t_exa_s[li]], writes=[tkT], dma=f"k{bi_}")
                voff = j * 8 * 128 * 128
                P.add("sp", o_dma(vv[:, 0:8, :], dap(exv_d[li], voff, [[128, 128], [128 * 128, 8], [1, 128]])), reads=[t_exv_d[li]], writes=[tvv], dma=f"v{bi_}")
                P.add("sp", o_dma(vv[:, 8:16, :], dap(exv_s[li], voff, [[128, 128], [128 * 128, 8], [1, 128]])), reads=[t_exv_s[li]], writes=[tvv], dma=f"v{bi_}")

                def ev_q(n, ps, tp, qq=qq, tqq=tqq):
                    sq, tsq = TMP()
                    sqb = sq[:, 0:256].bitcast(BF16)
                    P.add("act", o_act(sqb, ps[:, :], AF.Square), reads=[tp], writes=[tsq])
                    ps2, tp2 = PS()
                    P.add("pe", o_mm(ps2[:, :], bdones, sqb, True, True), reads=[tsq, t_cmb], writes=[tp2])
                    sd, tsd = TMP()
                    P.add("act", o_act(sd[:, 0:TW], ps2[:, :], AF.Sqrt, bias=eps_c, scale=1.0 / 64), reads=[tp2, t_cst], writes=[tsd])
                    P.add("dve", o_recip(sd[:, 0:TW], sd[:, 0:TW]), reads=[tsd], writes=[tsd])
                    P.add("dve", o_stt(qq[:, n * TW:(n + 1) * TW], ps[:, :], pc(80), sd[:, 0:TW], ALU.mult, ALU.mult),
                          reads=[tp, tsd, t_prm], writes=[tqq])
                proj_fm(layer, ("in", 2560 + j * 128), ev_q)

                for e2 in range(2):
                    h = 2 * j + e2
                    ct, tct = ct_b[e2], t_ct[e2]
                    for n in range(NT):
                        psc, tpc = PSLO()
                        P.add("pe", o_mm(psc[:, :], self_[:, h, :], qa8[:, n * TW:(n + 1) * TW], True, True), reads=[t_sel, t_qa8], writes=[tpc])
                        P.add("act", o_acopy(ct[:, n * TW:(n + 1) * TW], psc[:, :]), reads=[tpc], writes=[tct])
                blist = []
                for e2 in range(2):
                    for n in range(NT):
                        blocks = list(range(8)) + [8 + b_ for b_ in range(4 * n + 4)]
                        for bi2, blk in enumerate(blocks):
                            blist.append((e2, n, bi2, blk, len(blocks)))
                acc_state = {}
                st1 = {}

                def stage1(idx):
                    e2, n, bi2, blk, nb = blist[idx]
                    h = 2 * j + e2
                    R = slice(e2 * 64, e2 * 64 + 64)
                    ct, tct = ct_b[e2], t_ct[e2]
                    own = blk - 8
                    c0 = 0
                    diag = False
                    if own >= 4 * n:
                        c0 = (own - 4 * n) * 128
                        diag = True
                    ncol = TW - c0
                    pss, tpss = PSLO()
                    P.add("pe", o_mm(pss[:, 0:ncol], kT[R, blk * 128:(blk + 1) * 128], qq[R, n * TW + c0:(n + 1) * TW], True, True),
                          reads=[tkT, tqq], writes=[tpss])
                    z, tz = TMP()
                    P.add("dve", o_stt(z[:, 0:ncol], pss[:, 0:ncol], kcol[:, blk, h:h + 1], ct[:, n * TW + c0:(n + 1) * TW], ALU.add, ALU.add),
                          reads=[tpss, t_kcol, tct], writes=[tz])
                    pi = idx % NPT
                    pT, tpT = pT_b[pi], t_pT[pi]
                    P.add("act", o_act(pT[:, 0:ncol], z[:, 0:ncol], AF.Exp, scale=0.125), reads=[tz], writes=[tpT])
                    if diag:
                        P.add("dve", o_tt(pT[:, 0:128], pT[:, 0:128], mask_bf, ALU.mult), reads=[tpT, t_cmb], writes=[tpT])
                    st1[idx] = (pT, tpT, c0, ncol)

                def stage2(idx):
                    e2, n, bi2, blk, nb = blist[idx]
                    R = slice(e2 * 64, e2 * 64 + 64)
                    pT, tpT, c0, ncol = st1.pop(idx)
                    if bi2 == 0:
                        acc_state[(e2, n)] = (PSHI(), PSHI())
                    (pnum, tpn), (pden, tpd) = acc_state[(e2, n)]
                    first = bi2 == 0
                    last = bi2 == nb - 1
                    P.add("pe", o_mm(pnum[:, c0:TW], vv[:, blk, :], pT[:, 0:ncol], first, last), reads=[tvv, tpT], writes=[tpn])
                    P.add("pe", o_mm(pden[:, c0:TW], ones_bf, pT[:, 0:ncol], first, last), reads=[t_cmb, tpT], writes=[tpd])
                    if last:
                        rd_, trd = TMP()
                        P.add("dve", o_recip(rd_[R, 0:TW], pden[R, :]), reads=[tpd], writes=[trd])
                        P.add("dve", o_tt(ys[3][R, j, n * TW:(n + 1) * TW], pnum[R, :], rd_[R, 0:TW], ALU.mult), reads=[tpn, trd], writes=[t_y[3][j]])

                LOOK = 4
                nbl = len(blist)
                for idx in range(min(LOOK, nbl)):
                    stage1(idx)
                for idx in range(nbl):
                    stage2(idx)
                    if idx + LOOK < nbl:
                        stage1(idx + LOOK)
                    if idx % 3 == 2:
                        side_tick()
            while side:
                side_tick()
            dbg("ya", ys[0], t_y[0])
            dbg("yb", ys[1], t_y[1])
            dbg("yc", ys[2], t_y[2])
            dbg("yd", ys[3], t_y[3])

            if stop == 'attn':
                raise _Stop()
            merged = R2
            for j in range(16):
                wb, twb = load_w(layer, ("branch", j))
                accs = [TMP() for _ in range(NT)]
                for k in range(4):
                    wg, twg = load_w(layer, ("in", 4104 + k * 2048 + j * 128))
                    for n in range(NT):
                        psg, tpg = PS()
                        for kc in range(16):
                            P.add("pe", o_mm(psg[:, :], wg[:, kc, :], xn[:, kc, n * TW:(n + 1) * TW], kc == 0, kc == 15), reads=[twg, t_xn[n]], writes=[tpg])
                        psb_, tpb_ = PS()
                        for kc in range(4):
                            P.add("pe", o_mm(psb_[:, :], wb[:, k * 4 + kc, :], ys[k][:, kc, n * TW:(n + 1) * TW], kc == 0, kc == 3),
                                  reads=[twb, t_y[k][kc]], writes=[tpb_])
                        sg, tsg = TMP()
                        P.add("act", o_act(sg[:, 0:TW], psg[:, :], AF.Sigmoid), reads=[tpg], writes=[tsg])
                        acc, tacc = accs[n]
                        if k == 0:
                            P.add("dve", o_tt(acc[:, 0:TW], psb_[:, :], sg[:, 0:TW], ALU.mult), reads=[tpb_, tsg], writes=[tacc])
                        else:
                            P.add("dve", o_tt(sg[:, 0:TW], psb_[:, :], sg[:, 0:TW], ALU.mult), reads=[tpb_, tsg], writes=[tsg])
                            if k < 3:
                                P.add("dve", o_tt(acc[:, 0:TW], acc[:, 0:TW], sg[:, 0:TW], ALU.add), reads=[tacc, tsg], writes=[tacc])
                            else:
                                P.add("dve", o_tt(merged[:, j, n * TW:(n + 1) * TW], acc[:, 0:TW], sg[:, 0:TW], ALU.add), reads=[tacc, tsg], writes=[t_R2[j][n]])
            dbg("merged", merged[:], [t_R2[c][n] for c in range(16) for n in range(NT)])

            if stop == 'merge':
                raise _Stop()
            P.add("sp", o_dma(xT, xsp), reads=[t_xsp], writes=[t_xT[c][n] for c in range(16) for n in range(NT)], dma="unspill")

            for i in range(16):
                wo, two = load_w(layer, ("out", i))
                for n in range(NT):
                    ps, tp = PS()
                    for jc in range(16):
                        P.add("pe", o_mm(ps[:, :], wo[:, jc, :], merged[:, jc, n * TW:(n + 1) * TW], jc == 0, jc == 15), reads=[two, t_R2[jc][n]], writes=[tp])
                    P.add("dve", o_tt(xT[:, i, n * TW:(n + 1) * TW], xT[:, i, n * TW:(n + 1) * TW], ps[:, :], ALU.add), reads=[tp, t_xT[i][n]], writes=[t_xT[i][n]])

            if stop == 'wout':
                raise _Stop()
            rmsnorm_to_xn(li, 16)
            hq = R2
            for q in range(4):
                for hc in range(16):
                    wu, twu = load_w(layer, ("up", q * 16 + hc))
                    for n in range(NT):
                        ps, tp = PS()
                        for kc in range(16):
                            P.add("pe", o_mm(ps[:, :], wu[:, kc, :], xn[:, kc, n * TW:(n + 1) * TW], kc == 0, kc == 15), reads=[twu, t_xn[n]], writes=[tp])
                        r_, tr_ = TMP()
                        P.add("act", o_act(r_[:, 0:TW], ps[:, :], AF.Relu), reads=[tp], writes=[tr_])
                        P.add("dve", o_tt(hq[:, hc, n * TW:(n + 1) * TW], r_[:, 0:TW], r_[:, 0:TW], ALU.mult), reads=[tr_], writes=[t_R2[hc][n]])
                for i in range(16):
                    wd_, twd = load_w(layer, ("down", q, i))
                    for n in range(NT):
                        ps, tp = PS()
                        for hc in range(16):
                            P.add("pe", o_mm(ps[:, :], wd_[:, hc, :], hq[:, hc, n * TW:(n + 1) * TW], hc == 0, hc == 15), reads=[twd, t_R2[hc][n]], writes=[tp])
                        P.add("dve", o_tt(xT[:, i, n * TW:(n + 1) * TW], xT[:, i, n * TW:(n + 1) * TW], ps[:, :], ALU.add), reads=[tp, t_xT[i][n]], writes=[t_xT[i][n]])


        try:
            for li, layer in enumerate(layers):
                layer_body(li, layer)
        except _Stop:
            pass

        for c4 in range(4):
            P.add("sp", o_dma(yout[:, c4 * 4:(c4 + 1) * 4, :], xT[:, c4 * 4:(c4 + 1) * 4, :]),
                  reads=[t_xT[c][n] for c in range(c4 * 4, c4 * 4 + 4) for n in range(NT)], writes=[t_yout], dma="out")
        fw = ["out"]
        for name in dbg_list:
            fw.append("dbg_" + name)
        P.emit(final_waits=fw)
    return nc


_CACHE = {}


def make_in_maps(inp, x_override=None):
    W = pack_weights(inp)
    prm = pack_params(inp)
    cm, sel = shared_consts()
    x = np.asarray(inp["x"], np.float32) if x_override is None else x_override
    in_maps = []
    for core in range(8):
        b, half = core // 2, core % 2
        xs = x[b, half * T:(half + 1) * T, :]
        xTc = np.ascontiguousarray(xs.T.reshape(16, 128, T).transpose(1, 0, 2))
        in_maps.append({"xT": xTc, "W": W.reshape(DEPTH * NCH, 128, 16, 128), "prm": prm,
                        "cst": core_consts(half), "cmat": cm, "sel": sel})
    return in_maps


def gather_out(results, key="yT"):
    out = np.zeros((4, 2 * T, D), np.float32)
    for core in range(8):
        b, half = core // 2, core % 2
        yT = np.asarray(results[core][key])
        out[b, half * T:(half + 1) * T, :] = yT.transpose(1, 0, 2).reshape(D, T).T
    return out


def kernel(**inputs):
    inp = {k: np.asarray(v) for k, v in inputs.items()}
    if "nc" not in _CACHE:
        _CACHE["nc"] = build(layers=(0, 1))
    nc = _CACHE["nc"]
    in_maps = make_in_maps(inp)
    res = run_bass_kernel_spmd(nc, in_maps, core_ids=list(range(8)))
    return gather_out(res.results)
```
